# Optimizing a Trainium2 kernel written in Bass

```python
import math
import jax, jax.numpy as jnp
from jax import lax
import numpy as np

D_MODEL = 2048
BATCH = 4
SEQ = 4096
DEPTH = 1

HEAD_DIM = 64
BLOCK_Q = 128
SWA_HEADS = 16
SWA_KV_HEADS = 2
SWA_WINDOW = 128
NSA_HEADS = 16
NSA_KV_HEADS = 2
CMP_BLOCK = 32
CMP_STRIDE = 16
CMP_HIDDEN = 256
SEL_BLOCK = 64
SEL_TOPK = 16
SEL_LOCAL = 2
NSA_WINDOW = 512
REL_BUCKETS = 32
REL_MAX_DIST = 4096
REL_HEADS = SWA_HEADS + NSA_HEADS
MEM_LEN = 256
XA_HEADS = 4
XA_HEAD_DIM = 128
D_FF = 4 * D_MODEL
DN_ALPHA = (2.0 * DEPTH) ** 0.25
DN_BETA = (8.0 * DEPTH) ** -0.25
LN_EPS = 1e-5
NEG_INF = -1e30
FORCE_SCORE = 1e4

SPLIT_SIZES = (SWA_HEADS * HEAD_DIM, SWA_KV_HEADS * HEAD_DIM, SWA_KV_HEADS * HEAD_DIM, NSA_HEADS * HEAD_DIM,
               NSA_KV_HEADS * HEAD_DIM, NSA_KV_HEADS * HEAD_DIM, NSA_KV_HEADS * HEAD_DIM,
               NSA_KV_HEADS * HEAD_DIM, NSA_KV_HEADS * HEAD_DIM, NSA_KV_HEADS * HEAD_DIM,
               3 * NSA_HEADS, 2 * D_MODEL)
VALUE_COLS = (False, False, True, False, False, True, False, True, False, True, False, False)

kernel_name = "hybrid_swa_sink_nsa_gated_deepnorm"


def layer_norm(x, g, b):
    xf = x.astype(jnp.float32)
    mu = xf.mean(-1, keepdims=True)
    var = jnp.square(xf - mu).mean(-1, keepdims=True)
    y = (xf - mu) * lax.rsqrt(var + LN_EPS) * g.astype(jnp.float32) + b.astype(jnp.float32)
    return y.astype(x.dtype)


def rel_bucket(dist):
    exact = REL_BUCKETS // 2
    d = jnp.maximum(dist, 0)
    log_ratio = jnp.log(jnp.maximum(d, 1).astype(jnp.float32) / exact) / math.log(REL_MAX_DIST / exact)
    large = jnp.minimum(exact + (log_ratio * (REL_BUCKETS - exact)).astype(jnp.int32), REL_BUCKETS - 1)
    return jnp.where(d < exact, d, large)


def band_blocks(t, n_prev):
    B, S = t.shape[:2]
    nb = S // BLOCK_Q
    blk = t.reshape((B, nb, BLOCK_Q) + t.shape[2:])
    pad = jnp.zeros((B, n_prev) + blk.shape[2:], t.dtype)
    padded = jnp.concatenate([pad, blk], axis=1)
    return jnp.concatenate([padded[:, j:j + nb] for j in range(n_prev + 1)], axis=2)


def swa_sink_attention(q, k, v, sinks, rel_table):
    B, S = q.shape[:2]
    nb = S // BLOCK_Q
    G = SWA_KV_HEADS
    R = SWA_HEADS // G
    n_prev = -(-(SWA_WINDOW - 1) // BLOCK_Q)
    C = (n_prev + 1) * BLOCK_Q
    qb = q.reshape(B, nb, BLOCK_Q, G, R, HEAD_DIM) * (HEAD_DIM ** -0.5)
    kc = band_blocks(k, n_prev)
    vc = band_blocks(v, n_prev)
    logits = jnp.einsum('bnqgrd,bnkgd->bngrqk', qb, kc, preferred_element_type=jnp.float32)
    qi = jnp.arange(BLOCK_Q)
    kj = jnp.arange(C)
    dist = qi[:, None] + n_prev * BLOCK_Q - kj[None, :]
    bias = rel_table[rel_bucket(dist)].astype(jnp.float32).transpose(2, 0, 1).reshape(G, R, BLOCK_Q, C)
    kpos = jnp.arange(nb)[:, None, None] * BLOCK_Q - n_prev * BLOCK_Q + kj
    mask = (dist >= 0) & (dist < SWA_WINDOW) & (kpos >= 0)
    logits = jnp.where(mask[None, :, None, None], logits + bias, NEG_INF)
    sink = jnp.broadcast_to(sinks.astype(jnp.float32).reshape(1, 1, G, R, 1, 1), logits.shape[:-1] + (1,))
    probs = jax.nn.softmax(jnp.concatenate([logits, sink], axis=-1), axis=-1)[..., :-1]
    out = jnp.einsum('bngrqk,bnkgd->bnqgrd', probs.astype(v.dtype), vc)
    return out.reshape(B, S, SWA_HEADS * HEAD_DIM)


def compress_tokens(t, pos_emb, w1, w2):
    B, S, G = t.shape[:3]
    nc = (S - CMP_BLOCK) // CMP_STRIDE + 1
    idx = jnp.arange(nc)[:, None] * CMP_STRIDE + jnp.arange(CMP_BLOCK)[None, :]
    blocks = t[:, idx] + pos_emb[:, None, :]
    flat = blocks.transpose(0, 1, 3, 2, 4).reshape(B, nc, G, CMP_BLOCK * HEAD_DIM)
    return jax.nn.gelu(flat @ w1) @ w2


def nsa_attention(q, k_cmp, v_cmp, k_sel, v_sel, k_win, v_win, gate_logits,
                  cmp_pos_k, cmp_w1_k, cmp_w2_k, cmp_pos_v, cmp_w1_v, cmp_w2_v, rel_table):
    B, S = q.shape[:2]
    G = NSA_KV_HEADS
    R = NSA_HEADS // G
    qg = q.reshape(B, S, G, R, HEAD_DIM) * (HEAD_DIM ** -0.5)
    pos = jnp.arange(S)
    tbl = rel_table.reshape(REL_BUCKETS, G, R)

    kc = compress_tokens(k_cmp, cmp_pos_k, cmp_w1_k, cmp_w2_k)
    vc = compress_tokens(v_cmp, cmp_pos_v, cmp_w1_v, cmp_w2_v)
    nc = kc.shape[1]
    c_start = jnp.arange(nc) * CMP_STRIDE
    dist_c = pos[:, None] - (c_start + CMP_BLOCK - 1)[None, :]
    mask_c = dist_c >= 0
    bias_c = tbl[rel_bucket(dist_c)].astype(jnp.float32).transpose(2, 3, 0, 1)
    logits_c = jnp.einsum('bsgrd,bcgd->bgrsc', qg, kc, preferred_element_type=jnp.float32) + bias_c
    p_c = jax.nn.softmax(jnp.where(mask_c, logits_c, NEG_INF), axis=-1) * mask_c
    o_c = jnp.einsum('bgrsc,bcgd->bsgrd', p_c.astype(vc.dtype), vc)

    nsel = S // SEL_BLOCK
    s_start = jnp.arange(nsel) * SEL_BLOCK
    overlap = ((c_start[:, None] < s_start[None, :] + SEL_BLOCK) &
               (c_start[:, None] + CMP_BLOCK > s_start[None, :])).astype(jnp.float32)
    score = jnp.einsum('bgrsc,cj->bgsj', p_c, overlap)
    qblk = pos // SEL_BLOCK
    jsel = jnp.arange(nsel)
    causal = s_start[None, :] <= pos[:, None]
    back = qblk[:, None] - jsel[None, :]
    forced = (jsel[None, :] == 0) | ((back >= 0) & (back < SEL_LOCAL))
    score = jnp.where(causal, jnp.where(forced, FORCE_SCORE, score), -1.0)
    top = min(SEL_TOPK, nsel)
    top_val, top_idx = lax.top_k(score, top)
    top_ok = top_val >= 0

    nb = S // BLOCK_Q
    ks_blocks = k_sel.reshape(B, nsel, SEL_BLOCK, G, HEAD_DIM).transpose(0, 3, 1, 2, 4)
    vs_blocks = v_sel.reshape(B, nsel, SEL_BLOCK, G, HEAD_DIM).transpose(0, 3, 1, 2, 4)
    n_prev = -(-(NSA_WINDOW - 1) // BLOCK_Q)
    P = n_prev * BLOCK_Q
    kw_pad = jnp.pad(k_win, ((0, 0), (P, 0), (0, 0), (0, 0)))
    vw_pad = jnp.pad(v_win, ((0, 0), (P, 0), (0, 0), (0, 0)))
    b_ix = jnp.arange(B)[:, None, None, None]
    g_ix = jnp.arange(G)[None, :, None, None]

    def block_fn(args):
        i, q_i, idx_i, ok_i = args
        qpos = i * BLOCK_Q + jnp.arange(BLOCK_Q)
        kg = ks_blocks[b_ix, g_ix, idx_i]
        vg = vs_blocks[b_ix, g_ix, idx_i]
        kpos = idx_i[..., None] * SEL_BLOCK + jnp.arange(SEL_BLOCK)
        dist = qpos[:, None, None] - kpos
        mask = ok_i[..., None] & (dist >= 0)
        bias = tbl[rel_bucket(dist), g_ix[..., None]].astype(jnp.float32)
        logits = jnp.einsum('bqgrd,bgqnkd->bgrqnk', q_i, kg, preferred_element_type=jnp.float32)
        logits = jnp.where(mask[:, :, None], logits + bias.transpose(0, 1, 5, 2, 3, 4), NEG_INF)
        shp = logits.shape
        p = jax.nn.softmax(logits.reshape(shp[:4] + (-1,)), axis=-1).reshape(shp)
        o_s = jnp.einsum('bgrqnk,bgqnkd->bqgrd', p.astype(vg.dtype), vg)

        kw = lax.dynamic_slice_in_dim(kw_pad, i * BLOCK_Q, P + BLOCK_Q, axis=1)
        vw = lax.dynamic_slice_in_dim(vw_pad, i * BLOCK_Q, P + BLOCK_Q, axis=1)
        kpos_w = i * BLOCK_Q - P + jnp.arange(P + BLOCK_Q)
        dist_w = qpos[:, None] - kpos_w[None, :]
        mask_w = (dist_w >= 0) & (dist_w < NSA_WINDOW) & (kpos_w[None, :] >= 0)
        bias_w = tbl[rel_bucket(dist_w)].astype(jnp.float32).transpose(2, 3, 0, 1)
        logits_w = jnp.einsum('bqgrd,bkgd->bgrqk', q_i, kw, preferred_element_type=jnp.float32) + bias_w
        p_w = jax.nn.softmax(jnp.where(mask_w, logits_w, NEG_INF), axis=-1)
        o_w = jnp.einsum('bgrqk,bkgd->bqgrd', p_w.astype(vw.dtype), vw)
        return o_s, o_w

    q_blocks = qg.reshape(B, nb, BLOCK_Q, G, R, HEAD_DIM).transpose(1, 0, 2, 3, 4, 5)
    idx_blocks = top_idx.reshape(B, G, nb, BLOCK_Q, top).transpose(2, 0, 1, 3, 4)
    ok_blocks = top_ok.reshape(B, G, nb, BLOCK_Q, top).transpose(2, 0, 1, 3, 4)
    o_s, o_w = lax.map(block_fn, (jnp.arange(nb), q_blocks, idx_blocks, ok_blocks))
    o_s = o_s.transpose(1, 0, 2, 3, 4, 5).reshape(B, S, G, R, HEAD_DIM)
    o_w = o_w.transpose(1, 0, 2, 3, 4, 5).reshape(B, S, G, R, HEAD_DIM)

    g = jax.nn.sigmoid(gate_logits).reshape(B, S, G, R, 3)
    out = g[..., 0:1] * o_c + g[..., 1:2] * o_s + g[..., 2:3] * o_w
    return out.reshape(B, S, NSA_HEADS * HEAD_DIM)


def hybrid_mixer(x, w_in, attn_sinks, rel_table, cmp_pos_k, cmp_w1_k, cmp_w2_k, cmp_pos_v, cmp_w1_v, cmp_w2_v,
                 w_branch_swa, w_branch_nsa, w_mix_out):
    B, S, _ = x.shape
    h = x @ w_in
    offsets = np.cumsum(SPLIT_SIZES)[:-1].tolist()
    (q_a, k_a, v_a, q_b, k_c, v_c, k_s, v_s, k_w, v_w, g_nsa, g_merge) = jnp.split(h, offsets, axis=-1)
    kv = lambda t, n: t.reshape(B, S, n, HEAD_DIM)
    y_a = swa_sink_attention(kv(q_a, SWA_HEADS), kv(k_a, SWA_KV_HEADS), kv(v_a, SWA_KV_HEADS),
                             attn_sinks, rel_table[:, :SWA_HEADS])
    y_b = nsa_attention(kv(q_b, NSA_HEADS), kv(k_c, NSA_KV_HEADS), kv(v_c, NSA_KV_HEADS),
                        kv(k_s, NSA_KV_HEADS), kv(v_s, NSA_KV_HEADS), kv(k_w, NSA_KV_HEADS), kv(v_w, NSA_KV_HEADS),
                        g_nsa, cmp_pos_k, cmp_w1_k, cmp_w2_k, cmp_pos_v, cmp_w1_v, cmp_w2_v,
                        rel_table[:, SWA_HEADS:])
    g_a, g_b = jnp.split(g_merge, 2, axis=-1)
    merged = jax.nn.sigmoid(g_a) * (y_a @ w_branch_swa) + jax.nn.sigmoid(g_b) * (y_b @ w_branch_nsa)
    return merged @ w_mix_out


def memory_cross_attention(x, mem, w_q, w_kv, w_o):
    B, S, _ = x.shape
    M = mem.shape[1]
    q = (x @ w_q).reshape(B, S, XA_HEADS, XA_HEAD_DIM) * (XA_HEAD_DIM ** -0.5)
    k, v = jnp.split(mem @ w_kv, 2, axis=-1)
    k = k.reshape(B, M, XA_HEADS, XA_HEAD_DIM)
    v = v.reshape(B, M, XA_HEADS, XA_HEAD_DIM)
    p = jax.nn.softmax(jnp.einsum('bshd,bmhd->bhsm', q, k, preferred_element_type=jnp.float32), axis=-1)
    o = jnp.einsum('bhsm,bmhd->bshd', p.astype(v.dtype), v).reshape(B, S, XA_HEADS * XA_HEAD_DIM)
    return o @ w_o


def squared_relu_mlp(x, w1, w2):
    return jnp.square(jax.nn.relu(x @ w1)) @ w2


def setup_inputs(seed: int = 0) -> dict:
    key = jax.random.key(seed)
    keys = iter(jax.random.split(key, 64))
    f32 = jnp.float32
    L = DEPTH

    def normal(shape, std):
        return jax.random.normal(next(keys), shape, f32) * std

    x = normal((BATCH, SEQ, D_MODEL), 1.0)
    mem = normal((BATCH, MEM_LEN, D_MODEL), 1.0)
    w_in = jnp.concatenate([normal((L, D_MODEL, n), D_MODEL ** -0.5 * (DN_BETA if is_v else 1.0))
                            for n, is_v in zip(SPLIT_SIZES, VALUE_COLS)], axis=-1)
    attn_sinks = normal((L, SWA_HEADS), 0.5)
    rel_bias_table = normal((REL_BUCKETS, REL_HEADS), 0.5)
    cmp_pos_k = normal((L, CMP_BLOCK, HEAD_DIM), 0.1)
    cmp_w1_k = normal((L, CMP_BLOCK * HEAD_DIM, CMP_HIDDEN), (CMP_BLOCK * HEAD_DIM) ** -0.5)
    cmp_w2_k = normal((L, CMP_HIDDEN, HEAD_DIM), CMP_HIDDEN ** -0.5)
    cmp_pos_v = normal((L, CMP_BLOCK, HEAD_DIM), 0.1)
    cmp_w1_v = normal((L, CMP_BLOCK * HEAD_DIM, CMP_HIDDEN), (CMP_BLOCK * HEAD_DIM) ** -0.5)
    cmp_w2_v = normal((L, CMP_HIDDEN, HEAD_DIM), CMP_HIDDEN ** -0.5)
    w_branch_swa = normal((L, SWA_HEADS * HEAD_DIM, D_MODEL), DN_BETA * (SWA_HEADS * HEAD_DIM) ** -0.5)
    w_branch_nsa = normal((L, NSA_HEADS * HEAD_DIM, D_MODEL), DN_BETA * (NSA_HEADS * HEAD_DIM) ** -0.5)
    w_mix_out = normal((L, D_MODEL, D_MODEL), DN_BETA * D_MODEL ** -0.5)
    ln1_g = 1.0 + normal((L, D_MODEL), 0.02)
    ln1_b = normal((L, D_MODEL), 0.02)
    xa_w_q = normal((L, D_MODEL, XA_HEADS * XA_HEAD_DIM), D_MODEL ** -0.5)
    xa_w_kv = jnp.concatenate([normal((L, D_MODEL, XA_HEADS * XA_HEAD_DIM), D_MODEL ** -0.5),
                               normal((L, D_MODEL, XA_HEADS * XA_HEAD_DIM), DN_BETA * D_MODEL ** -0.5)], axis=-1)
    xa_w_o = normal((L, XA_HEADS * XA_HEAD_DIM, D_MODEL), DN_BETA * (XA_HEADS * XA_HEAD_DIM) ** -0.5)
    ln2_g = 1.0 + normal((L, D_MODEL), 0.02)
    ln2_b = normal((L, D_MODEL), 0.02)
    mlp_w1 = normal((L, D_MODEL, D_FF), DN_BETA * D_MODEL ** -0.5)
    mlp_w2 = normal((L, D_FF, D_MODEL), DN_BETA * D_FF ** -0.5)
    ln3_g = 1.0 + normal((L, D_MODEL), 0.02)
    ln3_b = normal((L, D_MODEL), 0.02)
    return {"x": x, "mem": mem, "w_in": w_in, "attn_sinks": attn_sinks, "rel_bias_table": rel_bias_table,
            "cmp_pos_k": cmp_pos_k, "cmp_w1_k": cmp_w1_k, "cmp_w2_k": cmp_w2_k,
            "cmp_pos_v": cmp_pos_v, "cmp_w1_v": cmp_w1_v, "cmp_w2_v": cmp_w2_v,
            "w_branch_swa": w_branch_swa, "w_branch_nsa": w_branch_nsa, "w_mix_out": w_mix_out,
            "ln1_g": ln1_g, "ln1_b": ln1_b, "xa_w_q": xa_w_q, "xa_w_kv": xa_w_kv, "xa_w_o": xa_w_o,
            "ln2_g": ln2_g, "ln2_b": ln2_b, "mlp_w1": mlp_w1, "mlp_w2": mlp_w2,
            "ln3_g": ln3_g, "ln3_b": ln3_b}


def reference(x, mem, w_in, attn_sinks, rel_bias_table, cmp_pos_k, cmp_w1_k, cmp_w2_k, cmp_pos_v, cmp_w1_v,
              cmp_w2_v, w_branch_swa, w_branch_nsa, w_mix_out, ln1_g, ln1_b, xa_w_q, xa_w_kv, xa_w_o,
              ln2_g, ln2_b, mlp_w1, mlp_w2, ln3_g, ln3_b):
    h = x
    for l in range(DEPTH):
        mix = hybrid_mixer(h, w_in[l], attn_sinks[l], rel_bias_table, cmp_pos_k[l], cmp_w1_k[l], cmp_w2_k[l],
                           cmp_pos_v[l], cmp_w1_v[l], cmp_w2_v[l], w_branch_swa[l], w_branch_nsa[l], w_mix_out[l])
        h = layer_norm(DN_ALPHA * h + mix, ln1_g[l], ln1_b[l])
        xa = memory_cross_attention(h, mem, xa_w_q[l], xa_w_kv[l], xa_w_o[l])
        h = layer_norm(DN_ALPHA * h + xa, ln2_g[l], ln2_b[l])
        ff = squared_relu_mlp(h, mlp_w1[l], mlp_w2[l])
        h = layer_norm(DN_ALPHA * h + ff, ln3_g[l], ln3_b[l])
    return h
```

```python
import math
import os
from contextlib import ExitStack

import numpy as np
import ml_dtypes
import concourse.bass as bass
import concourse.mybir as mybir
from concourse.bass_utils import run_bass_kernel_spmd

F32 = mybir.dt.float32
BF16 = mybir.dt.bfloat16
AF = mybir.ActivationFunctionType
ALU = mybir.AluOpType

D = 2048
S = 4096
NB = 32
NEG = -30000.0
ALPHA = 2.0 ** 0.25
EPS = 1e-5
C_QA, C_KA, C_VA, C_QB, C_KC, C_VC, C_KS, C_VS, C_KW, C_VW, C_GN, C_GA, C_GB = (
    0, 1024, 1152, 1280, 2304, 2432, 2560, 2688, 2816, 2944, 3072, 3120, 5168)
L_SEL = 4224
L_CMP = 6144
L_SWA = 384
L_WIN = 768
OFFC = 2063

DEBUG = {}
UPTO = int(os.environ.get('K_UPTO', '9'))


class _Op:
    __slots__ = ("eng", "fn", "deps", "signal", "count", "dma", "semkey", "semval", "flushed")

    def __init__(self, eng, fn, dma, semkey):
        self.eng = eng
        self.fn = fn
        self.deps = []
        self.signal = False
        self.count = None
        self.dma = dma
        self.semkey = semkey
        self.semval = None
        self.flushed = False


class Prog:
    ENGS = ("pe", "act", "dve", "pool", "sp")

    def __init__(self, nc, stack):
        self.nc = nc
        self.stack = stack
        self.ops = {e: [] for e in self.ENGS}
        self.lastw = {}
        self.readers = {}
        self.semcount = {}
        self.dsem = {}
        self.nblk = 0
        self.pool_dmas = []
        self.esem = None
        self.ebase = None

    def _dep(self, op, other, kind):
        if other is None or other is op or other.flushed:
            return
        if not other.dma and not op.dma and other.eng == op.eng:
            if kind != "RAW" or op.eng == "pe":
                return
        if other not in op.deps:
            op.deps.append(other)
            other.signal = True

    def op(self, eng, fn, reads=(), writes=(), dma=False, semkey=None):
        o = _Op(eng, fn, dma, semkey)
        for r in reads:
            self._dep(o, self.lastw.get(r), "RAW")
        for w in writes:
            self._dep(o, self.lastw.get(w), "WAW")
            for rd in self.readers.get(w, ()):
                self._dep(o, rd, "WAR")
        for r in reads:
            self.readers.setdefault(r, []).append(o)
        for w in writes:
            self.lastw[w] = o
            self.readers[w] = []
        if dma:
            if semkey is None:
                semkey = writes[0]
                o.semkey = semkey
            self.semcount[semkey] = self.semcount.get(semkey, 0) + 16
            o.semval = self.semcount[semkey]
        self.ops[eng].append(o)
        return o

    def dma(self, q, out, in_, reads=(), writes=(), semkey=None, **kw):
        o = self.op(q, lambda e: e.dma_start(out=out, in_=in_, **kw), reads, writes, dma=True, semkey=semkey)
        if q == "pool":
            self.pool_dmas.append(o)
            if len(self.pool_dmas) > 3:
                prev = self.pool_dmas[-4]
                if not prev.flushed and prev not in o.deps:
                    o.deps.append(prev)
        return o

    def mm(self, out, lhsT, rhs, start, stop, reads=(), writes=()):
        return self.op("pe", lambda e: e.matmul(out, lhsT, rhs, start=start, stop=stop), reads, writes)

    def tr(self, out, in_, ident, reads=(), writes=()):
        return self.op("pe", lambda e: e.transpose(out, in_, ident), reads, writes)

    def flush(self):
        nc = self.nc
        self.nblk += 1
        if self.esem is None or max(self.ebase.values()) > 45000:
            self.esem = {e: self.stack.enter_context(nc.semaphore("es%d_%s" % (self.nblk, e))) for e in self.ENGS}
            self.ebase = {e: 0 for e in self.ENGS}
        esem = self.esem
        for k in self.semcount:
            if k not in self.dsem:
                self.dsem[k] = self.stack.enter_context(nc.semaphore("ds%d" % len(self.dsem)))
        dsem = self.dsem
        maxc = {}
        for e in self.ENGS:
            c = self.ebase[e]
            for o in self.ops[e]:
                if o.signal and not o.dma:
                    c += 1
                    o.count = c
            maxc[e] = c if c > self.ebase[e] else 0
            self.ebase[e] = c
        if os.environ.get("K_VERBOSE"):
            print("flush", self.nblk, {e: len(self.ops[e]) for e in self.ENGS}, dict(self.ebase), len(self.dsem))
        final = [(dsem[k], v) for k, v in self.semcount.items()]
        ops = self.ops

        def run(ename, eng, last=False):
            waited = {}
            for o in ops[ename]:
                for d in o.deps:
                    if d.flushed:
                        continue
                    if d.dma:
                        s, v = dsem[d.semkey], d.semval
                    else:
                        s, v = esem[d.eng], d.count
                    key = id(s)
                    if waited.get(key, 0) >= v:
                        continue
                    waited[key] = v
                    eng.wait_ge(s, v)
                ins = o.fn(eng)
                if o.dma:
                    ins.then_inc(dsem[o.semkey], 16)
                elif o.signal:
                    ins.then_inc(esem[ename], 1)
            if last:
                for s, v in final:
                    eng.wait_ge(s, v)
                for e2 in self.ENGS:
                    if e2 != ename and maxc[e2]:
                        eng.wait_ge(esem[e2], maxc[e2])

        with nc.Block() as block:
            @block.tensor
            def _(eng):
                run("pe", eng)

            @block.scalar
            def _(eng):
                run("act", eng)

            @block.vector
            def _(eng):
                run("dve", eng)

            @block.gpsimd
            def _(eng):
                run("pool", eng)

            @block.sync
            def _(eng):
                run("sp", eng, last=True)

        for e in self.ENGS:
            for o in self.ops[e]:
                o.flushed = True
                o.fn = None
                o.deps = None
        self.ops = {e: [] for e in self.ENGS}
        self.lastw = {}
        self.readers = {}


def _bucket(d):
    d = np.maximum(d, 0)
    lr = np.log(np.maximum(d, 1).astype(np.float32) / np.float32(16)) / np.float32(math.log(4096 / 16))
    large = np.minimum(16 + (lr * np.float32(16)).astype(np.int32), 31)
    return np.where(d < 16, d, large)


def _bucket_exact(d):
    import jax
    import jax.numpy as jnp
    with jax.default_device(jax.devices("cpu")[0]):
        dd = jnp.asarray(d, dtype=jnp.int32)
        exact = 16
        dm = jnp.maximum(dd, 0)
        log_ratio = jnp.log(jnp.maximum(dm, 1).astype(jnp.float32) / exact) / math.log(4096 / exact)
        large = jnp.minimum(exact + (log_ratio * 16).astype(jnp.int32), 31)
        return np.asarray(jnp.where(dm < exact, dm, large))


def _onehot_table(L, off, valid_fn):
    i = np.arange(L)
    d = i - off
    valid = valid_fn(d)
    bk = _bucket_exact(d)
    oh = np.zeros((33, L), np.float32)
    oh[bk[valid], i[valid]] = 1.0
    oh[32, i[~valid]] = 1.0
    return oh


def _host_consts(h):
    c = {}
    c["oh_swa"] = _onehot_table(L_SWA, 127, lambda d: (d >= 0) & (d < 128))
    c["oh_sel"] = _onehot_table(L_SEL, 127, lambda d: d >= 0)
    c["oh_win"] = _onehot_table(L_WIN, 127, lambda d: (d >= 0) & (d < 512))
    c["oh_cmp"] = _onehot_table(L_CMP, OFFC, lambda d: d >= 0)
    memb = np.zeros((128, 32, 128), np.float32)
    for l in range(32):
        memb[2 * l, l, 0:64] = 1.0
        memb[2 * l + 1, l, 64:128] = 1.0
    c["memb"] = memb
    ov = np.zeros((128, 2, 65), np.float32)
    for cc in range(255):
        for j in range(64):
            if 16 * cc < 64 * j + 64 and 16 * cc + 32 > 64 * j:
                ov[cc % 128, cc // 128, j] = 1.0
        ov[cc % 128, cc // 128, 64] = 1.0
    c["ov"] = ov
    npad = 1 - h
    c["padb"] = np.full((128, 1), NEG * npad, np.float32)
    cv = np.zeros((128, 1), np.float32)
    if npad:
        cv[0:8, 0] = NEG
    c["cvalid"] = cv
    smul = np.zeros((128, 16, 64), np.float32)
    sadd = np.zeros((128, 16, 64), np.float32)
    q = np.arange(128)[:, None]
    j = np.arange(64)[None, :]
    for t in range(16):
        pos = (2 * t + 1) * 128 + q
        causal = (64 * j <= pos)
        pad = j < 2 * npad
        back = pos // 64 - j
        forced = (j == 2 * npad) | ((back >= 0) & (back < 2))
        ok = causal & ~pad
        smul[:, t, :] = (ok & ~forced).astype(np.float32)
        sadd[:, t, :] = np.where(ok, np.where(forced, 1e4, 0.0), -1.0)
    c["smul"] = smul
    c["sadd"] = sadd
    return c


class K:
    pass


def build(debug=()):
    nc = bass.Bass("TRN2", target_bir_lowering=False)
    k = K()
    k.nc = nc
    k.debug = set(debug)
    k.dbg_out = {}

    def din(name, shape, dt=F32):
        return nc.dram_tensor(name, list(shape), dt, kind="ExternalInput").ap()

    k.xq = din("xq", [2048, D])
    k.xf = din("xf", [S, D])
    k.mem = din("mem", [256, D])
    k.w_in = din("w_in", [D, 7216])
    k.sinks = din("attn_sinks", [1, 16])
    k.rel = din("rel_bias_table", [32, 32])
    k.cpos = {"k": din("cmp_pos_k", [32, 64]), "v": din("cmp_pos_v", [32, 64])}
    k.cw1 = {"k": din("cmp_w1_k", [2048, 256]), "v": din("cmp_w1_v", [2048, 256])}
    k.cw2 = {"k": din("cmp_w2_k", [256, 64]), "v": din("cmp_w2_v", [256, 64])}
    k.wba = din("w_branch_swa", [1024, D])
    k.wbb = din("w_branch_nsa", [1024, D])
    k.wmix = din("w_mix_out", [D, D])
    k.lng = [din("ln%d_g" % i, [1, D]) for i in (1, 2, 3)]
    k.lnb = [din("ln%d_b" % i, [1, D]) for i in (1, 2, 3)]
    k.xwq = din("xa_w_q", [D, 512])
    k.xwkv = din("xa_w_kv", [D, 1024])
    k.xwo = din("xa_w_o", [512, D])
    k.w1 = din("mlp_w1", [D, 8192])
    k.w2 = din("mlp_w2", [8192, D])
    k.oh = {n: din("oh_" + n, [33, L]) for n, L in (("swa", L_SWA), ("sel", L_SEL), ("win", L_WIN), ("cmp", L_CMP))}
    k.memb_d = din("memb", [128, 32, 128])
    k.ov_d = din("ov", [128, 2, 65])
    k.padb_d = din("padb", [128, 1])
    k.cvalid_d = din("cvalid", [128, 1])
    k.smul_d = din("smul", [128, 16, 64])
    k.sadd_d = din("sadd", [128, 16, 64])
    k.out = nc.dram_tensor("out", [2048, D], F32, kind="ExternalOutput").ap()
    k.G = {n: nc.dram_tensor("G_" + n, [32, L], BF16, kind="Internal") for n, L in
           (("swa", L_SWA), ("sel", L_SEL), ("win", L_WIN), ("cmp", L_CMP))}
    k.R = {n: nc.dram_tensor("R_" + n, [16, 16, L], BF16, kind="Internal") for n, L in
           (("swa", L_SWA), ("sel", L_SEL), ("win", L_WIN), ("cmp", L_CMP))}
    k.YT = nc.dram_tensor("YT", [16, 128, 2048], BF16, kind=("ExternalOutput" if "YT" in k.debug else "Internal"))
    wb_specs = [("ga", k.w_in[:, C_GA:C_GA + 2048], 2048, 2048, 512), ("wba", k.wba, 1024, 2048, 512),
                ("gb", k.w_in[:, C_GB:C_GB + 2048], 2048, 2048, 512), ("wbb", k.wbb, 1024, 2048, 512),
                ("wmix", k.wmix, 2048, 2048, 512), ("xwq", k.xwq, 2048, 512, 2048), ("xwo", k.xwo, 512, 2048, 512),
                ("w1", k.w1, 2048, 8192, 128), ("w2", k.w2, 8192, 2048, 512)]
    k.Wb = {}
    k.cast_jobs = []
    for name, src, rows, cols, rb in wb_specs:
        t_ = nc.dram_tensor("Wb_" + name, [rows, cols], BF16, kind="Internal")
        k.Wb[name] = t_.ap()
        for r0 in range(0, rows, rb):
            k.cast_jobs.append((t_.ap()[r0:r0 + rb, :], src[r0:r0 + rb, :]))

    with ExitStack() as top:
        k.top = top
        P = Prog(nc, top)
        k.P = P

        def sb(st, name, shape, dt):
            return st.enter_context(nc.sbuf_tensor(name, list(shape), dt))

        k.sb = sb
        k.PSall = top.enter_context(nc.psum_tensor("psall", [128, 8, 512], F32))
        k.PS = [k.PSall[:, i, :] for i in range(8)]
        k.psi = 0
        k.nrot = 8
        k.ws = [sb(top, "ws%d" % i, [128, 16, 512], BF16) for i in range(3)]
        k.wsi = 0
        k.ident = sb(top, "ident", [128, 128], BF16)
        k.identf = sb(top, "identf", [128, 128], F32)
        k.evi = 0
        k.ncast = 0

        P.op("pool", lambda e: e.memset(k.identf[:], 0.0), writes=["identf"])
        P.op("pool", lambda e: e.affine_select(out=k.identf[:], in_=k.identf[:], pattern=[[-1, 128]],
                                               compare_op=ALU.not_equal, fill=1.0, base=0, channel_multiplier=1),
             reads=["identf"], writes=["identf"])
        P.op("dve", lambda e: e.tensor_copy(k.ident[:], k.identf[:]), reads=["identf"], writes=["ident"])

        with ExitStack() as sA:
            k.sA = sA
            k.KT = {n: [sb(sA, "KT_%s%d" % (n, g), [128, S], BF16) for g in range(2)] for n in ("swa", "sel", "win")}
            k.V = {n: sb(sA, "V_" + n, [128, NB, 2, 65], BF16) for n in ("swa", "sel", "win")}
            k.kcT = [sb(sA, "kcT%d" % g, [128, 256], BF16) for g in range(2)]
            k.VO = sb(sA, "VO", [128, 2, 2, 129], BF16)
            with ExitStack() as sB:
                k.cmpT = {n: sb(sB, "cmpT_" + n, [128, S], BF16) for n in ("k", "v")}
                with ExitStack() as s1:
                    stage_kv(k, s1)
                    P.flush()
                if UPTO >= 2:
                    with ExitStack() as s1:
                        stage_cmp(k, s1)
                        P.flush()
            if UPTO >= 3:
                stage_attn(k)
        if UPTO >= 4:
            stage_dense(k)
    return nc, k


def bank(k):
    n = k.nrot
    i = k.psi % n
    k.psi = (k.psi + 1) % n
    return k.PS[i], "ps%d" % i


def wslot(k):
    i = k.wsi
    k.wsi = (k.wsi + 1) % 3
    return k.ws[i], "ws%d" % i


def evac(k, out, in_, reads, writes, scale=None, eng=None):
    if eng is None:
        eng = "dve" if (k.evi % 2 == 0) else "act"
        k.evi += 1
    if eng == "dve":
        if scale is None:
            k.P.op("dve", lambda e: e.tensor_copy(out, in_), reads, writes)
        else:
            k.P.op("dve", lambda e: e.tensor_scalar(out=out, in0=in_, scalar1=float(scale), scalar2=None, op0=ALU.mult),
                   reads, writes)
    else:
        if scale is None:
            k.P.op("act", lambda e: e.activation(out=out, in_=in_, func=AF.Copy), reads, writes)
        else:
            k.P.op("act", lambda e: e.activation(out=out, in_=in_, func=AF.Copy, scale=float(scale)), reads, writes)


def transpose_block(k, dst, srcs, reads, wkey, eng=None):
    P = k.P
    ps, pk = bank(k)
    psb = ps[:].bitcast(BF16)
    n = len(srcs)
    for i, s in enumerate(srcs):
        P.tr(psb[:, i * 128:(i + 1) * 128], s, k.ident[:], reads=list(reads) + ["ident"], writes=[pk])
    evac(k, dst, psb[:, 0:n * 128], [pk], wkey if isinstance(wkey, list) else [wkey], eng=eng)


def dbg_dump(k, name, ap_sb, shape, dt, reads):
    if name not in k.debug:
        return
    t = k.nc.dram_tensor("dbg_" + name, list(shape), dt, kind="ExternalOutput").ap()
    k.dbg_out[name] = t
    k.P.dma("sp", t, ap_sb, reads=reads, writes=["dbg_" + name])


def stage_kv(k, st):
    nc, P, sb = k.nc, k.P, k.sb
    WkT = sb(st, "WkT", [128, 16, 8, 128], BF16)
    Wv = sb(st, "Wv", [128, 16, 384], BF16)
    xb = sb(st, "xb1", [128, 2, D], BF16)
    xT = sb(st, "xT1", [128, 16, 256], BF16)
    wv = k.w_in.rearrange("(kc p) c -> p kc c", p=128)
    kblocks = [("swa", 0, C_KA), ("swa", 1, C_KA + 64), ("sel", 0, C_KS), ("sel", 1, C_KS + 64),
               ("win", 0, C_KW), ("win", 1, C_KW + 64)]
    P.dma("pool", xb[:, :, :], k.xf[0:256, :].rearrange("(t p) d -> p t d", p=128), writes=["xb1"])
    stg, stgk = wslot(k)
    stage = stg[:, :, :].rearrange("p a b -> p (a b)")[:, 0:16 * 384].rearrange("p (kc c) -> p kc c", c=384)
    for i, c0 in enumerate((C_KA, C_KS, C_KW)):
        P.dma("pool", stage[:, :, i * 128:(i + 1) * 128], wv[:, :, c0:c0 + 128], writes=[("stg", i)])
    wkeys = []
    for bi, (n_, g_, _) in enumerate(kblocks):
        i = bi // 2
        for hh in range(2):
            key = ("WkT", bi, hh)
            wkeys.append(key)
            eng_ = "dve" if hh == 0 else "pool"
            P.op(eng_, (lambda bi, hh, i, g_: lambda e: e.tensor_copy(WkT[:, :, bi, hh * 64:(hh + 1) * 64], stage[:, :, i * 128 + g_ * 64:i * 128 + g_ * 64 + 64]))(bi, hh, i, g_),
                 reads=[("stg", i)], writes=[key])
    for bi, c0 in ((6, C_KC), (7, C_VC)):
        key = ("WkT", bi, 0)
        wkeys.append(key)
        P.dma("pool", WkT[:, :, bi, :], wv[:, :, c0:c0 + 128], writes=[key])
    vkeys = []
    for i, c0 in enumerate((C_VA, C_VS, C_VW)):
        key = ("Wv", i)
        vkeys.append(key)
        P.dma("pool", Wv[:, :, i * 128:(i + 1) * 128], wv[:, :, c0:c0 + 128], writes=[key])
    for n in ("swa", "sel", "win"):
        P.op("pool", (lambda n: lambda e: e.memset(k.V[n][:, :, :, 64:65], 1.0))(n), writes=[("V1", n)])
    tab = sb(st, "tab", [33, 32], BF16)
    P.dma("pool", tab[0:32, :], k.rel, writes=[("tab", 0)])
    P.op("dve", lambda e: e.memset(tab[32:33, :], NEG), writes=[("tab", 1)])
    ohs = sb(st, "ohs", [33, 1024], BF16)
    gsb = sb(st, "gsb", [32, 1024], BF16)

    def table_work():
        for n, L in (("cmp", L_CMP), ("sel", L_SEL), ("win", L_WIN), ("swa", L_SWA)):
            Gt = k.G[n]
            for c0 in range(0, L, 1024):
                w = min(1024, L - c0)
                P.dma("pool", ohs[:, 0:w], k.oh[n][:, c0:c0 + w], writes=["ohs"], max_dma_last_dim=4096)
                for s0 in range(0, w, 512):
                    ww = min(512, w - s0)
                    ps, pk = bank(k)
                    P.mm(ps[0:32, 0:ww], tab[:, :], ohs[:, s0:s0 + ww], True, True, reads=["ohs", ("tab", 0), ("tab", 1)], writes=[pk])
                    evac(k, gsb[:, s0:s0 + ww], ps[0:32, 0:ww], [pk], [("gsb", s0)], eng="dve")
                P.dma("sp", Gt.ap()[:, c0:c0 + w], gsb[:, 0:w], reads=[("gsb", 0), ("gsb", 512)], writes=[("G", n, c0)], semkey=("Gw", n, c0 % 2048))
                yield
            h0 = 0 if n == "swa" else 16
            for hh in range(16):
                src = bass.AP(Gt, (h0 + hh) * L, [[0, 16], [1, L]])
                P.dma("sp", k.R[n].ap()[hh], src, reads=[("G", n, c0) for c0 in range(0, L, 1024)], writes=[("R", n, hh)], semkey="Rtab")
        while True:
            yield
    tw = table_work()
    for c in range(int(os.environ.get('K_NCH', '16'))):
        t0 = c * 256
        next(tw)
        if c > 0:
            P.dma("pool", xb[:, :, :], k.xf[t0:t0 + 256, :].rearrange("(t p) d -> p t d", p=128), writes=["xb1"])
        MODE = os.environ.get('K_MODE', 'ABCD')
        for kp in range(8 if 'B' in MODE else 0):
            srcs = [xb[:, tt, (2 * kp + kk) * 128:(2 * kp + kk + 1) * 128] for kk in range(2) for tt in range(2)]
            transpose_block(k, xT[:, 2 * kp:2 * kp + 2, :], srcs, ["xb1"], ("xT1", kp))
        xkeys = [("xT1", kp) for kp in range(8)]
        for bi in range(8 if 'C' in MODE else 0):
            ps, pk = bank(k)
            for kc in range(16):
                P.mm(ps[:, 0:256], WkT[:, kc, bi, :], xT[:, kc, :], kc == 0, kc == 15,
                     reads=[("xT1", kc // 2)] + (wkeys if kc == 0 else []), writes=[pk])
            if bi < 6:
                n, g, _ = kblocks[bi]
                dst, dk = k.KT[n][g][:, t0:t0 + 256], ("KT", n, g, c)
            else:
                n = "k" if bi == 6 else "v"
                dst, dk = k.cmpT[n][:, t0:t0 + 256], ("cmpT", n, c)
            evac(k, dst, ps[:, 0:256], [pk], [dk])
        for tt in range(2 if 'D' in MODE else 0):
            ps, pk = bank(k)
            for kc in range(16):
                P.mm(ps[:, 0:384], xT[:, kc, tt * 128:(tt + 1) * 128], Wv[:, kc, :], kc == 0, kc == 15,
                     reads=[("xT1", kc // 2)] + (vkeys if kc == 0 else []), writes=[pk])
            blk = 2 * c + tt
            for i, n in enumerate(("swa", "sel", "win")):
                evac(k, k.V[n][:, blk, :, 0:64],
                     ps[:, i * 128:(i + 1) * 128].rearrange("p (g d) -> p g d", g=2), [pk], [("V", n, blk)],
                     eng=os.environ.get("K_VENG", "dve"))
    if "KT_sel0" in k.debug:
        dbg_dump(k, "KT_sel0", k.KT["sel"][0][:], [128, S], BF16, [("KT", "sel", 0, c) for c in range(16)])
    if "V_win" in k.debug:
        dbg_dump(k, "V_win", k.V["win"][:], [128, NB, 2, 65], BF16,
                 [("V", "win", b) for b in range(NB)] + [("V1", "win")])
    if "cmpT_k" in k.debug:
        dbg_dump(k, "cmpT_k", k.cmpT["k"][:], [128, S], BF16, [("cmpT", "k", c) for c in range(16)])


def stage_cmp(k, st):
    nc, P, sb = k.nc, k.P, k.sb
    cmp_reads = {n: [("cmpT", n, c) for c in range(16)] for n in ("k", "v")}
    w1 = {n: sb(st, "w1" + n, [128, 32, 256], BF16) for n in ("k", "v")}
    w2k = sb(st, "w2k", [128, 2, 128], BF16)
    w2v = sb(st, "w2v", [128, 2, 64], BF16)
    posT = {n: sb(st, "posT" + n, [128, 32], BF16) for n in ("k", "v")}
    hb = sb(st, "hbias", [128, 4], F32)
    hT = {(n, g): sb(st, "hT%s%d" % (n, g), [128, 2, 256], BF16) for n in ("k", "v") for g in range(2)}
    xs = sb(st, "gx", [128, 256], F32)
    u = sb(st, "gu", [128, 256], F32)
    sg = sb(st, "gs", [128, 256], F32)
    for n in ("k", "v"):
        src = k.cw1[n].rearrange("(l d) h -> d l h", d=64)
        for hh in range(2):
            P.dma("pool", w1[n][hh * 64:(hh + 1) * 64, :, :], src, writes=[("w1", n, hh)])
        for hh in range(2):
            P.dma("pool", posT[n][hh * 64:(hh + 1) * 64, :], k.cpos[n].rearrange("l d -> d l"),
                  writes=[("posT", n, hh)], allow_slow_non_contiguous=True)
    for hh in range(2):
        P.dma("pool", w2k[:, :, hh * 64:(hh + 1) * 64], k.cw2["k"].rearrange("(c p) d -> p c d", p=128),
              writes=[("w2k", hh)])
    P.dma("pool", w2v[:, :, :], k.cw2["v"].rearrange("(c p) d -> p c d", p=128), writes=[("w2v",)])
    P.dma("pool", k.VO[:, :, 0, 64:129], k.ov_d, writes=[("VOc", 0)])
    P.dma("pool", k.VO[:, :, 1, 64:129], k.ov_d, writes=[("VOc", 1)])
    for key in hT:
        P.op("pool", (lambda t: lambda e: e.memset(t[:], 0.0))(hT[key]), writes=[("hT", key)])
    for ni, n in enumerate(("k", "v")):
        for hc in range(2):
            ps, pk = bank(k)
            for l in range(32):
                P.mm(ps[:, 0:1], w1[n][0:64, l, hc * 128:(hc + 1) * 128], posT[n][0:64, l:l + 1], l == 0, l == 31,
                     reads=[("w1", n, 0), ("posT", n, 0)], writes=[pk])
            evac(k, hb[:, 2 * ni + hc:2 * ni + hc + 1], ps[:, 0:1], [pk], [("hb", ni, hc)], eng="dve")
    for ni, n in enumerate(("k", "v")):
        for g in range(2):
            for hc in range(2):
                ps, pk = bank(k)
                for l in range(32):
                    P.mm(ps[:, 0:255], w1[n][g * 64:(g + 1) * 64, l, hc * 128:(hc + 1) * 128],
                         k.cmpT[n][g * 64:(g + 1) * 64, l:l + 16 * 254 + 1:16], l == 0, l == 31,
                         reads=[("w1", n, g)] + (cmp_reads[n] if l == 0 else []), writes=[pk])
                bcol = hb[:, 2 * ni + hc:2 * ni + hc + 1]
                X, U, SG = xs[:, 0:255], u[:, 0:255], sg[:, 0:255]
                P.op("dve", (lambda ps, bcol, X: lambda e: e.tensor_scalar(out=X, in0=ps[:, 0:255], scalar1=bcol, scalar2=None, op0=ALU.add))(ps, bcol, X),
                     reads=[pk, ("hb", ni, hc)], writes=["gx"])
                P.op("dve", (lambda X, U: lambda e: e.tensor_tensor(out=U, in0=X, in1=X, op=ALU.mult))(X, U), reads=["gx"], writes=["gu"])
                P.op("dve", (lambda U: lambda e: e.tensor_scalar(out=U, in0=U, scalar1=0.044715, scalar2=1.0, op0=ALU.mult, op1=ALU.add))(U),
                     reads=["gu"], writes=["gu"])
                P.op("dve", (lambda X, U: lambda e: e.tensor_tensor(out=U, in0=U, in1=X, op=ALU.mult))(X, U), reads=["gu", "gx"], writes=["gu"])
                P.op("act", (lambda U, SG: lambda e: e.activation(out=SG, in_=U, func=AF.Sigmoid, scale=1.5957691216057308))(U, SG),
                     reads=["gu"], writes=["gs"])
                dst = hT[(n, g)][:, hc, 0:255]
                P.op("dve", (lambda X, SG, dst: lambda e: e.tensor_tensor(out=dst, in0=X, in1=SG, op=ALU.mult))(X, SG, dst),
                     reads=["gx", "gs", ("hT", (n, g))], writes=[("hT", (n, g), hc)])
    for g in range(2):
        hkeys = [("hT", ("k", g), hc) for hc in range(2)] + [("hT", ("k", g))]
        ps, pk = bank(k)
        for hc in range(2):
            P.mm(ps[:, 0:256], w2k[:, hc, :], hT[("k", g)][:, hc, :], hc == 0, hc == 1,
                 reads=hkeys + [("w2k", 0), ("w2k", 1)], writes=[pk])
        evac(k, k.kcT[g][:, :], ps[:, 0:256], [pk], [("kcT", g)], eng="dve")
        hkeys = [("hT", ("v", g), hc) for hc in range(2)] + [("hT", ("v", g))]
        for cb in range(2):
            ps, pk = bank(k)
            for hc in range(2):
                P.mm(ps[:, 0:64], hT[("v", g)][:, hc, cb * 128:(cb + 1) * 128], w2v[:, hc, :], hc == 0, hc == 1,
                     reads=hkeys + [("w2v",)], writes=[pk])
            evac(k, k.VO[:, cb, g, 0:64], ps[:, 0:64], [pk], [("VOv", cb, g)], eng="dve")
    if "kcT0" in k.debug:
        dbg_dump(k, "kcT0", k.kcT[0][:], [128, 256], BF16, [("kcT", 0)])
    if "VO" in k.debug:
        dbg_dump(k, "VO", k.VO[:], [128, 2, 2, 129], BF16,
                 [("VOv", cb, g) for cb in range(2) for g in range(2)] + [("VOc", 0), ("VOc", 1)])


def stage_attn(k):
    nc, P, sb = k.nc, k.P, k.sb
    k.nrot = 4
    k.psi = 0
    OB = [(k.PS[4 + s_], "ps%d" % (4 + s_)) for s_ in range(4)]
    oset = [0]
    with ExitStack() as st:
        Wswa = sb(st, "Wswa", [128, 16, 256], BF16)
        Wwin4 = sb(st, "Wwin4", [128, 16, 128], BF16)
        memb = sb(st, "membs", [128, 32, 128], BF16)
        padb = sb(st, "padbs", [128, 1], F32)
        cvalid = sb(st, "cvalids", [128, 1], F32)
        sinkexp = sb(st, "sinkexp", [128, 16], F32)
        Wg = sb(st, "Wg", [128, 16, 48], BF16)
        xb = sb(st, "xb2", [128, 2, D], BF16)
        xTq = sb(st, "xTq", [128, 16, 256], BF16)
        QT = sb(st, "QT", [128, 16, 2, 256], BF16)
        gate = sb(st, "gate", [128, 2, 48], F32)
        ya = sb(st, "ya", [128, 2, 1024], BF16)
        yb = sb(st, "yb", [128, 2, 1024], BF16)
        Wc = [sb(st, "Wc%d" % i, [128, 16, 128], BF16) for i in range(2)]
        PT = [sb(st, "PT%d" % i, [128, 2, 256], BF16) for i in range(4)]
        score = [sb(st, "score%d" % g, [128, 64], F32) for g in range(2)]
        smul = sb(st, "smuls", [128, 2, 64], F32)
        sadd = sb(st, "sadds", [128, 2, 64], F32)
        s1 = sb(st, "s1", [128, 64], F32)
        wk = sb(st, "wk", [128, 64], F32)
        m8 = sb(st, "m8", [128, 8], F32)
        selm = sb(st, "selm", [128, 64], F32)
        negsel = sb(st, "negsel", [128, 64], BF16)
        negselT = sb(st, "negselT", [128, 2, 2, 128], BF16)
        negsel4 = sb(st, "negsel4", [128, 2, 2, 64], BF16)
        dn = sb(st, "dn", [128, 8], F32)
        dn2 = sb(st, "dn2", [128, 16], F32)
        pti = [0]
        dni = [0]
        wc_keys = {}

        Lw, Ls, Lc, La = L_WIN, L_SEL, L_CMP, L_SWA
        def toeplitz(dst, rname, L, base, pstep, nh, h0, ncols, wkey, plain=None):
            for a in range(8):
                src = bass.AP(k.R[rname], h0 * 16 * L + base - pstep * 16 * a, [[L - pstep, 16], [16 * L, nh], [1, ncols]])
                P.dma("sp", dst[16 * a:16 * a + 16, :, 0:ncols], src,
                      writes=[(wkey, a)] + ([plain] if (plain is not None and a == 0) else []), semkey=(wkey, a % 2))
            return [(wkey, a) for a in range(8)] + ([plain] if plain is not None else [])

        swa_keys = toeplitz(Wswa, "swa", La, 127, 1, 16, 0, 256, "Wswa")
        win_keys = toeplitz(Wwin4, "win", Lw, 127 + 512, 1, 16, 0, 128, "Wwin4")
        P.dma("pool", memb[:, :, :], k.memb_d, writes=["memb"], max_dma_last_dim=2048)
        P.op("dve", lambda e: e.memset(QT[:, :, :, :], 0.0), writes=["QTz"])
        P.op("dve", lambda e: e.memset(negselT[:, :, :, :], 0.0), writes=["nsz"])
        P.dma("sp", padb[:, :], k.padb_d, writes=["padb"])
        P.dma("sp", cvalid[:, :], k.cvalid_d, writes=["cvalid"])
        P.dma("sp", sinkexp[:, :], k.sinks.broadcast_to([128, 16]), writes=["sinkexp"])
        P.op("act", lambda e: e.activation(out=sinkexp[:, :], in_=sinkexp[:, :], func=AF.Exp), reads=["sinkexp"], writes=["sinkexp"])
        P.dma("pool", Wg[:, :, :], k.w_in.rearrange("(kc p) c -> p kc c", p=128)[:, :, C_GN:C_GN + 48], writes=["Wg"])
        wv = k.w_in.rearrange("(kc p) c -> p kc c", p=128)

        pend = []

        def run_pending(keep=0):
            while len(pend) > keep:
                pend.pop(0)()

        spi = [0]

        def attn_unit(kind, tt, t, j, qblk, g, lset, KTt, Vt, ncol_o, dcol, wfn, bias_fn, mask, fin):
            ob = OB[oset[0]]
            oset[0] = (oset[0] + 1) % 4
            nl = len(lset)
            groups = []
            i = 0
            while i < nl:
                if (kind != "cmp" and i + 1 < nl and bias_fn(lset[i]) is None and bias_fn(lset[i + 1]) is None):
                    groups.append([i, i + 1])
                    i += 2
                else:
                    groups.append([i])
                    i += 1
            for grp in groups:
                if len(grp) == 2:
                    p_ = spi[0]
                    spi[0] ^= 1
                    Sb = [(k.PS[2 * p_], "ps%d" % (2 * p_)), (k.PS[2 * p_ + 1], "ps%d" % (2 * p_ + 1))]
                    S_in = k.PSall[:, 2 * p_:2 * p_ + 2, 0:256]
                else:
                    Sb = [bank(k)]
                    S_in = Sb[0][0][:, 0:256]
                sks = [b[1] for b in Sb]
                for gi_, li in enumerate(grp):
                    l = lset[li]
                    S_, sk = Sb[gi_]
                    for hh in range(2):
                        P.mm(S_[:, hh * 128:(hh + 1) * 128], KTt[:, l * 128:(l + 1) * 128], QT[:, qblk, hh, tt * 128:(tt + 1) * 128], hh == 0, False,
                             reads=[("QT", qblk), "QTz"], writes=[sk])
                    wt, wkeys = wfn(l)
                    P.mm(S_[:, 0:256].rearrange("p (h q) -> p h q", h=2), k.ident[:, :], wt, False, mask is None,
                         reads=wkeys, writes=[sk])
                    if mask is not None:
                        for hh in range(2):
                            P.mm(S_[:, hh * 128:(hh + 1) * 128], memb[:, l, :], negselT[:, g, tt, :], False, hh == 1,
                                 reads=["memb", ("negselT", g, tt), "nsz"], writes=[sk])
                pt = PT[pti[0]]
                ptk = "PT%d" % pti[0]
                pti[0] = (pti[0] + 1) % len(PT)
                ng = len(grp)
                pt_out = pt[:, 0:ng, :] if ng == 2 else pt[:, 0, :]
                bap = bias_fn(lset[grp[0]]) if ng == 1 else None
                if bap is None:
                    P.op("act", (lambda pt_out, S_in: lambda e: e.activation(out=pt_out, in_=S_in, func=AF.Exp))(pt_out, S_in),
                         reads=sks, writes=[ptk])
                else:
                    P.op("act", (lambda pt_out, S_in, bap: lambda e: e.activation(out=pt_out, in_=S_in, func=AF.Exp, bias=bap[0]))(pt_out, S_in, bap),
                         reads=sks + [bap[1]], writes=[ptk])
                run_pending(1)

                def pv(grp=grp, pt=pt, ptk=ptk):
                    for gi_, li in enumerate(grp):
                        l = lset[li]
                        for hh in range(2):
                            P.mm(ob[0][:, hh * ncol_o:(hh + 1) * ncol_o], pt[:, gi_, hh * 128:(hh + 1) * 128], Vt[:, l, g, 0:ncol_o],
                                 li == 0 and hh == 0, li == nl - 1 and hh == 1, reads=[ptk], writes=[ob[1]])
                    if grp[-1] == nl - 1:
                        fin(ob)
                pend.append(pv)

        def small2():
            i = dni[0]
            dni[0] = (dni[0] + 1) % 8
            return dn2[:, 2 * i:2 * i + 2], ("dn2", i)

        def small():
            i = dni[0]
            dni[0] = (dni[0] + 1) % 8
            return dn[:, i:i + 1], ("dn", i)

        for c in range(int(os.environ.get("K_NCHA", "8"))):
            tok0 = c * 256
            P.dma("pool", xb[:, :, :], k.xq[tok0:tok0 + 256, :].rearrange("(t p) d -> p t d", p=128), writes=["xb2"])
            P.dma("sp", smul[:, :, :], k.smul_d[:, 2 * c:2 * c + 2, :], writes=["smul"])
            P.dma("sp", sadd[:, :, :], k.sadd_d[:, 2 * c:2 * c + 2, :], writes=["sadd"])
            for kp in range(8):
                srcs = [xb[:, tt, (2 * kp + kk) * 128:(2 * kp + kk + 1) * 128] for kk in range(2) for tt in range(2)]
                transpose_block(k, xTq[:, 2 * kp:2 * kp + 2, :], srcs, ["xb2"], ("xTq", kp))
            xkeys = [("xTq", kp) for kp in range(8)]
            AT = os.environ.get('K_AT', 'qgscny')
            for wb, c0 in enumerate((C_QA, C_QA + 512, C_QB, C_QB + 512) if 'q' in AT else ()):
                wt, wkey = wslot(k)
                P.dma("pool", wt[:, :, :], wv[:, :, c0:c0 + 512], writes=[wkey])
                for sbk in range(4):
                    ps, pk = bank(k)
                    for kc in range(16):
                        P.mm(ps[:, 0:256], wt[:, kc, sbk * 128:(sbk + 1) * 128], xTq[:, kc, :], kc == 0, kc == 15,
                             reads=[("xTq", kc // 2), wkey], writes=[pk])
                    qb = wb * 4 + sbk
                    evac(k, QT[0:64, qb, 0, :], ps[0:64, 0:256], [pk, "QTz"], [("QT", qb)], scale=0.125)
                    evac(k, QT[64:128, qb, 1, :], ps[64:128, 0:256], [pk, "QTz"], [("QT", qb)], scale=0.125)
            for tt in range(2 if 'g' in AT else 0):
                ps, pk = bank(k)
                for kc in range(16):
                    P.mm(ps[:, 0:48], xTq[:, kc, tt * 128:(tt + 1) * 128], Wg[:, kc, :], kc == 0, kc == 15,
                         reads=[("xTq", kc // 2), "Wg"], writes=[pk])
                P.op("act", (lambda ps, tt: lambda e: e.activation(out=gate[:, tt, :], in_=ps[:, 0:48], func=AF.Sigmoid))(ps, tt),
                     reads=[pk], writes=[("gate", tt)])
            for _ in range(7):
                if k.cast_jobs:
                    dst_, src_ = k.cast_jobs.pop(0)
                    ci_ = k.ncast
                    k.ncast += 1
                    P.dma("pool", dst_, src_, writes=[("Wbc", ci_)], semkey=("Wbc", ci_ % 4), max_dma_last_dim=8192)
            for tt in range(2 if 'c' in AT else 0):
                t = 2 * c + tt
                lq = 2 * t + 1
                cbs = [0] if lq < 16 else [0, 1]
                for cb in cbs:
                    base = OFFC - 31 + 128 * lq - 2048 * cb
                    wc_keys[cb] = toeplitz(Wc[cb], "cmp", Lc, base, 16, 16, 0, 128, ("Wc", cb))
                for j in range(8):
                    g = j // 4
                    def fin_cmp(ob, j=j, tt=tt, g=g):
                        O_, ok_ = ob
                        O3 = O_[:, 0:258].rearrange("p (h c) -> p h c", h=2)
                        d1, dk1 = small2()
                        d2, dk2 = small2()
                        P.op("dve", lambda e: e.tensor_scalar(out=d1, in0=O3[:, :, 128], scalar1=1e-30, scalar2=None, op0=ALU.max), reads=[ok_], writes=[dk1])
                        P.op("dve", lambda e: e.reciprocal(out=d1, in_=d1), reads=[dk1], writes=[dk1])
                        P.op("dve", lambda e: e.tensor_tensor(out=d2, in0=d1, in1=gate[:, tt, 6 * j:6 * j + 4:3], op=ALU.mult),
                             reads=[dk1, ("gate", tt)], writes=[dk2])
                        for hh in range(2):
                            hd = 2 * j + hh
                            P.op("dve", (lambda hd, hh: lambda e: e.tensor_scalar(out=yb[:, tt, hd * 64:(hd + 1) * 64], in0=O3[:, hh, 0:64], scalar1=d2[:, hh:hh + 1], scalar2=None, op0=ALU.mult))(hd, hh),
                                 reads=[ok_, dk2], writes=[("yb", tt, hd)])
                            if j % 4 == 0 and hh == 0:
                                P.op("dve", (lambda hh: lambda e: e.tensor_scalar(out=score[g][:, :], in0=O3[:, hh, 64:128], scalar1=d1[:, hh:hh + 1], scalar2=None, op0=ALU.mult))(hh),
                                     reads=[ok_, dk1], writes=[("score", g)])
                            else:
                                P.op("dve", (lambda hh: lambda e: e.scalar_tensor_tensor(out=score[g][:, :], in0=O3[:, hh, 64:128], scalar=d1[:, hh:hh + 1], in1=score[g][:, :], op0=ALU.mult, op1=ALU.add))(hh),
                                     reads=[ok_, dk1, ("score", g)], writes=[("score", g)])
                    attn_unit("cmp", tt, t, j, 8 + j, g, cbs, k.kcT[g], k.VO, 129, 128,
                              lambda l, j=j: (Wc[l][:, 2 * j:2 * j + 2, :], wc_keys[l]),
                              lambda l: ((cvalid[:, 0:1], "cvalid") if l == 0 else None), None, fin_cmp)
                run_pending()
                for g in range(2):
                    P.op("dve", (lambda g, tt: lambda e: e.tensor_tensor(out=s1[:, :], in0=score[g][:, :], in1=smul[:, tt, :], op=ALU.mult))(g, tt),
                         reads=[("score", g), "smul"], writes=["s1"])
                    P.op("dve", (lambda tt: lambda e: e.tensor_tensor(out=s1[:, :], in0=s1[:, :], in1=sadd[:, tt, :], op=ALU.add))(tt),
                         reads=["s1", "sadd"], writes=["s1"])
                    P.op("dve", lambda e: e.max(out=m8[:, :], in_=s1[:, :]), reads=["s1"], writes=["m8"])
                    P.op("dve", lambda e: e.match_replace(out=wk[:, :], in_to_replace=m8[:, :], in_values=s1[:, :], imm_value=-1e9),
                         reads=["s1", "m8"], writes=["wk"])
                    P.op("dve", lambda e: e.max(out=m8[:, :], in_=wk[:, :]), reads=["wk"], writes=["m8"])
                    P.op("dve", lambda e: e.tensor_scalar(out=selm[:, :], in0=s1[:, :], scalar1=m8[:, 7:8], scalar2=None, op0=ALU.is_ge),
                         reads=["s1", "m8"], writes=["selm"])
                    P.op("dve", lambda e: e.scalar_tensor_tensor(out=selm[:, :], in0=s1[:, :], scalar=0.0, in1=selm[:, :], op0=ALU.is_ge, op1=ALU.mult),
                         reads=["s1", "selm"], writes=["selm"])
                    P.op("dve", lambda e: e.tensor_scalar(out=negsel[:, :], in0=selm[:, :], scalar1=-1.0, scalar2=-NEG, op0=ALU.add, op1=ALU.mult),
                         reads=["selm"], writes=["negsel"])
                    if "selm" in k.debug and c == int(os.environ.get("K_DBGC", "0")):
                        dbg_dump(k, "selm_%d_%d" % (tt, g), selm[:, :], [128, 64], F32, ["selm"])
                        k.debug.add("selm_%d_%d" % (tt, g))
                        dbg_dump(k, "selm_%d_%d" % (tt, g), selm[:, :], [128, 64], F32, ["selm"])
                    P.op("dve", (lambda g, tt: lambda e: e.tensor_copy(negsel4[:, tt, g, :], negsel[:, :]))(g, tt), reads=["negsel"], writes=[("negsel4", tt, g)])
            for tt in range(2 if 's' in AT else 0):
                t = 2 * c + tt
                lq = 2 * t + 1
                for j in range(8):
                    g = j // 4
                    def fin_swa(ob, j=j, tt=tt):
                        O_, ok_ = ob
                        O3 = O_[:, 0:130].rearrange("p (h c) -> p h c", h=2)
                        d1, dk1 = small2()
                        P.op("dve", lambda e: e.tensor_tensor(out=d1, in0=O3[:, :, 64], in1=sinkexp[:, 2 * j:2 * j + 2], op=ALU.add),
                             reads=[ok_, "sinkexp"], writes=[dk1])
                        P.op("dve", lambda e: e.reciprocal(out=d1, in_=d1), reads=[dk1], writes=[dk1])
                        for hh in range(2):
                            hd = 2 * j + hh
                            P.op("dve", (lambda hd, hh: lambda e: e.tensor_scalar(out=ya[:, tt, hd * 64:(hd + 1) * 64], in0=O3[:, hh, 0:64], scalar1=d1[:, hh:hh + 1], scalar2=None, op0=ALU.mult))(hd, hh),
                                 reads=[ok_, dk1], writes=[("ya", tt, hd)])
                    attn_unit("swa", tt, t, j, j, g, [lq - 1, lq], k.KT["swa"][g], k.V["swa"], 65, 64,
                              lambda l, j=j, lq=lq: (Wswa[:, 2 * j:2 * j + 2, (lq - l) * 128:(lq - l + 1) * 128], swa_keys),
                              lambda l: ((padb[:, 0:1], "padb") if l == 0 else None), None, fin_swa)
            run_pending()
            for tt in range(2 if 'c' in AT else 0):
                for g in range(2):
                    ps, pk = bank(k)
                    psb = ps[:].bitcast(BF16)
                    P.tr(psb[0:64, 0:128], negsel4[:, tt, g, :], k.ident[:, :], reads=[("negsel4", tt, g), "ident"], writes=[pk])
                    evac(k, negselT[0:64, g, tt, :], psb[0:64, 0:128], [pk, "nsz"], [("negselT", g, tt)], eng="dve")
            lqmax = 2 * (2 * c + 1) + 1
            ncols = 128 * (lqmax + 1)
            for j in range(8 if 'n' in AT else 0):
                g = j // 4
                wt, wkey = wslot(k)
                wsel = wt[:, :, :].rearrange("p a b -> p (a b)").rearrange("p (h u) -> p h u", h=2)
                wkeys_sel = toeplitz(wsel, "sel", Ls, 127, 1, 2, 2 * j, ncols, ("T", wkey), plain=wkey)
                for tt in range(2):
                    t = 2 * c + tt
                    lq = 2 * t + 1
                    for br, (kind, lset) in enumerate((("sel", list(range(0, lq + 1))), ("win", list(range(max(0, lq - 4), lq + 1))))):
                        if kind == "sel":
                            wfn = lambda l, lq=lq, wsel=wsel, wkey=wkeys_sel: (wsel[:, :, (lq - l) * 128:(lq - l + 1) * 128], wkey)
                            bfn = lambda l: None
                            mask = True
                        else:
                            wfn = (lambda l, lq=lq, wsel=wsel, wkey=wkeys_sel, j=j:
                                   ((wsel[:, :, (lq - l) * 128:(lq - l + 1) * 128], wkey) if lq - l < 4
                                    else (Wwin4[:, 2 * j:2 * j + 2, :], win_keys)))
                            bfn = lambda l: ((padb[:, 0:1], "padb") if l == 0 else None)
                            mask = None
                        def fin_sw(ob, j=j, tt=tt, br=br):
                            O_, ok_ = ob
                            O3 = O_[:, 0:130].rearrange("p (h c) -> p h c", h=2)
                            d1, dk1 = small2()
                            P.op("dve", lambda e: e.reciprocal(out=d1, in_=O3[:, :, 64]), reads=[ok_], writes=[dk1])
                            P.op("dve", lambda e: e.tensor_tensor(out=d1, in0=d1, in1=gate[:, tt, 6 * j + 1 + br:6 * j + 5 + br:3], op=ALU.mult),
                                 reads=[dk1, ("gate", tt)], writes=[dk1])
                            for hh in range(2):
                                hd = 2 * j + hh
                                P.op("dve", (lambda hd, hh: lambda e: e.scalar_tensor_tensor(out=yb[:, tt, hd * 64:(hd + 1) * 64], in0=O3[:, hh, 0:64], scalar=d1[:, hh:hh + 1], in1=yb[:, tt, hd * 64:(hd + 1) * 64], op0=ALU.mult, op1=ALU.add))(hd, hh),
                                     reads=[ok_, dk1, ("yb", tt, hd)], writes=[("yb", tt, hd)])
                        attn_unit(kind, tt, t, j, 8 + j, g, lset, k.KT[kind][g], k.V[kind], 65, 64, wfn, bfn, mask, fin_sw)
            run_pending()
            for bp in range(8 if 'y' in AT else 0):
                src_t = ya if bp < 4 else yb
                b0 = (bp % 4) * 2
                srcs = [src_t[:, tt, (b0 + kk) * 128:(b0 + kk + 1) * 128] for kk in range(2) for tt in range(2)]
                rk = [("ya" if bp < 4 else "yb", tt, hd) for tt in range(2) for hd in range(2 * b0, 2 * b0 + 4)]
                transpose_block(k, xTq[:, 2 * bp:2 * bp + 2, :], srcs, rk, ("xTq", bp))
            P.dma("sp", k.YT.ap().rearrange("b p t -> p b t")[:, :, tok0:tok0 + 256], xTq[:, :, :],
                  reads=[("xTq", bp) for bp in range(8)], writes=["YT"])
            if c == 7:
                P.flush()
    k.nrot = 8
    k.psi = 0


def lin_fm(k, srcs, Wd, c0, ncols, T, evac_fn, q="sp"):
    P = k.P
    Wv = Wd.rearrange("(kc p) c -> p kc c", p=128)
    ng = len(srcs)
    for cb in range(ncols // 512):
        banks = [bank(k) for _ in range(4)]
        kc0 = 0
        for gi, (src, skeys, nkc) in enumerate(srcs):
            wt, wkey = wslot(k)
            P.dma(q, wt[:, 0:nkc, :], Wv[:, kc0:kc0 + nkc, c0 + cb * 512:c0 + (cb + 1) * 512], writes=[wkey])
            for sbk in range(4):
                ps, pk = banks[sbk]
                for kc in range(nkc):
                    P.mm(ps[:, 0:T], wt[:, kc, sbk * 128:(sbk + 1) * 128], src[:, kc, 0:T],
                         gi == 0 and kc == 0, gi == ng - 1 and kc == nkc - 1,
                         reads=[wkey] + (skeys if kc == 0 else []), writes=[pk])
            kc0 += nkc
        for sbk in range(4):
            evac_fn(cb * 4 + sbk, banks[sbk][0], banks[sbk][1])


def lin_tm(k, srcs, Wd, c0, ncols, ntt, evac_fn, q="sp"):
    P = k.P
    Wv = Wd.rearrange("(kc p) c -> p kc c", p=128)
    ng = len(srcs)
    for cb in range(ncols // 512):
        banks = [bank(k) for _ in range(ntt)]
        kc0 = 0
        for gi, (src, skeys, nkc) in enumerate(srcs):
            wt, wkey = wslot(k)
            P.dma(q, wt[:, 0:nkc, :], Wv[:, kc0:kc0 + nkc, c0 + cb * 512:c0 + (cb + 1) * 512], writes=[wkey])
            for tt in range(ntt):
                ps, pk = banks[tt]
                for kc in range(nkc):
                    P.mm(ps[:, 0:512], src[:, kc, tt * 128:(tt + 1) * 128], wt[:, kc, :],
                         gi == 0 and kc == 0, gi == ng - 1 and kc == nkc - 1,
                         reads=[wkey] + (skeys if kc == 0 else []), writes=[pk])
            kc0 += nkc
        for tt in range(ntt):
            evac_fn(tt, cb, banks[tt][0], banks[tt][1])


def stage_dense(k):
    nc, P, sb = k.nc, k.P, k.sb
    k.nrot = 8
    k.psi = 0
    with ExitStack() as st:
        G = [sb(st, "G%d" % i, [128, 16, 512], BF16) for i in range(5)]
        h = sb(st, "h", [128, 4, D], F32)
        lnp = sb(st, "lnp", [128, 2, D], F32)
        stats = sb(st, "stats", [128, 4, 6], F32)
        mv = sb(st, "mv", [128, 2], F32)
        rs = sb(st, "rs", [128, 1], F32)
        tmpf = [sb(st, "tmpf%d" % i, [128, 512], F32) for i in range(2)]
        tmpb = [sb(st, "tmpb%d" % i, [128, 512], BF16) for i in range(2)]
        qT = sb(st, "qTx", [128, 4, 512], BF16)
        PTx = [sb(st, "PTx%d" % i, [128, 512], BF16) for i in range(4)]
        otm = sb(st, "otm", [128, 4, 512], BF16)
        oT = sb(st, "oTx", [128, 4, 512], BF16)
        memKT = sb(st, "memKT", [128, 4, 256], BF16)
        memV = sb(st, "memV", [128, 2, 4, 130], BF16)
        rd = sb(st, "rd", [128, 4], F32)

        def KG(i):
            return [("G", i, kc) for kc in range(16)]

        def Gview(i, t):
            return G[i][:, :, :].rearrange("p a b -> p (a b)").rearrange("p (t d) -> p t d", t=t)

        memb16 = Gview(0, 4)
        P.dma("pool", memb16[:, 0:2, :], k.mem.rearrange("(t p) d -> p t d", p=128), writes=KG(0))
        memT = G[1][:, :, 0:256]
        for kp in range(8):
            srcs = [memb16[:, tt, (2 * kp + kk) * 128:(2 * kp + kk + 1) * 128] for kk in range(2) for tt in range(2)]
            transpose_block(k, memT[:, 2 * kp:2 * kp + 2, :], srcs, KG(0), [("G", 1, 2 * kp), ("G", 1, 2 * kp + 1)])
        mkeys = KG(1)
        lin_fm(k, [(memT, mkeys, 16)], k.xwkv, 0, 512, 256,
               lambda nb, ps, pk: evac(k, memKT[:, nb, :], ps[:, 0:256], [pk], [("memKT", nb)]), q="pool")
        P.op("dve", lambda e: e.memset(memV[:, :, :, 128:130], 1.0), writes=["memV1"])
        lin_tm(k, [(memT, mkeys, 16)], k.xwkv, 512, 512, 2,
               lambda tt, cb, ps, pk: evac(k, memV[:, tt, :, 0:128], ps[:, 0:512].rearrange("p (h d) -> p h d", h=4),
                                           [pk], [("memV", tt)], eng="dve"), q="pool")
        mkv_keys = [("memKT", nb) for nb in range(4)] + [("memV", tt) for tt in range(2)] + ["memV1"]

        tfi = [0]

        def layer_norm(i, c, last):
            tok0 = c * 512
            P.dma("sp", lnp[:, 0, :], k.lng[i].broadcast_to([128, D]), writes=[("lnp", 0)])
            P.dma("sp", lnp[:, 1, :], k.lnb[i].broadcast_to([128, D]), writes=[("lnp", 1)])
            hb = Gview(0, 4)
            for tt in range(4):
                hk = ("h", tt)
                for jj in range(4):
                    P.op("dve", (lambda tt, jj: lambda e: e.bn_stats(out=stats[:, jj, :], in_=h[:, tt, jj * 512:(jj + 1) * 512]))(tt, jj),
                         reads=[hk], writes=[("stats", jj)])
                P.op("dve", lambda e: e.bn_aggr(out=mv[:, :], in_=stats[:, :, :].rearrange("p a b -> p (a b)")),
                     reads=[("stats", jj) for jj in range(4)], writes=["mv"])
                P.op("dve", lambda e: e.tensor_scalar(out=rs[:, :], in0=mv[:, 1:2], scalar1=EPS, scalar2=None, op0=ALU.add),
                     reads=["mv"], writes=["rs"])
                P.op("act", lambda e: e.activation(out=rs[:, :], in_=rs[:, :], func=AF.Sqrt), reads=["rs"], writes=["rs"])
                P.op("dve", lambda e: e.reciprocal(out=rs[:, :], in_=rs[:, :]), reads=["rs"], writes=["rs"])
                P.op("dve", (lambda tt: lambda e: e.scalar_tensor_tensor(out=h[:, tt, :], in0=h[:, tt, :], scalar=mv[:, 0:1], in1=lnp[:, 0, :],
                                                                        op0=ALU.subtract, op1=ALU.mult))(tt),
                     reads=[hk, "mv", ("lnp", 0)], writes=[hk])
                P.op("dve", (lambda tt: lambda e: e.scalar_tensor_tensor(out=h[:, tt, :], in0=h[:, tt, :], scalar=rs[:, 0:1], in1=lnp[:, 1, :],
                                                                        op0=ALU.mult, op1=ALU.add))(tt),
                     reads=[hk, "rs", ("lnp", 1)], writes=[hk])
                if last:
                    P.dma("sp", k.out[tok0 + tt * 128:tok0 + (tt + 1) * 128, :], h[:, tt, :], reads=[hk],
                          writes=[("out", c, tt)], semkey=("out", tt))
                else:
                    P.op("act", (lambda tt: lambda e: e.activation(out=hb[:, tt, :], in_=h[:, tt, :], func=AF.Copy))(tt),
                         reads=[hk], writes=[("G", 0, 4 * tt + q_) for q_ in range(4)])
            if not last:
                for kc in range(16):
                    transpose_block(k, G[1][:, kc, :], [hb[:, tt, kc * 128:(kc + 1) * 128] for tt in range(4)], KG(0), ("G", 1, kc))

        def resid_evac(tt, cb, ps, pk):
            P.op("dve", (lambda tt, cb, ps: lambda e: e.scalar_tensor_tensor(
                out=h[:, tt, cb * 512:(cb + 1) * 512], in0=h[:, tt, cb * 512:(cb + 1) * 512], scalar=ALPHA, in1=ps[:, 0:512],
                op0=ALU.mult, op1=ALU.add))(tt, cb, ps), reads=[pk, ("h", tt)], writes=[("h", tt)])

        for c in range(int(os.environ.get("K_NCHD", "4"))):
            tok0 = c * 512
            xsrc = k.xq[tok0:tok0 + 512, :].rearrange("(t p) d -> p t d", p=128)
            xb = Gview(0, 4)
            P.dma("pool", xb[:, :, :], xsrc, writes=KG(0))
            P.dma("sp", h[:, :, :], xsrc, writes=[("h", tt) for tt in range(4)])
            P.dma("sp", G[2][:, :, :], k.YT.ap().rearrange("b p t -> p b t")[:, :, tok0:tok0 + 512], writes=KG(2))
            for kc in range(16):
                transpose_block(k, G[1][:, kc, :], [xb[:, tt, kc * 128:(kc + 1) * 128] for tt in range(4)], KG(0), ("G", 1, kc))
            lin_fm(k, [(G[1], KG(1), 16)], k.Wb["ga"], 0, 2048, 512,
                   lambda nb, ps, pk: P.op("act", lambda e: e.activation(out=G[3][:, nb, :], in_=ps[:, 0:512], func=AF.Sigmoid),
                                           reads=[pk], writes=[("G", 3, nb)]))
            lin_fm(k, [(G[2][:, 0:8, :], [("G", 2, i) for i in range(8)], 8)], k.Wb["wba"], 0, 2048, 512,
                   lambda nb, ps, pk: P.op("dve", lambda e: e.tensor_tensor(out=G[3][:, nb, :], in0=ps[:, 0:512], in1=G[3][:, nb, :], op=ALU.mult),
                                           reads=[pk, ("G", 3, nb)], writes=[("G", 3, nb)]))
            lin_fm(k, [(G[1], KG(1), 16)], k.Wb["gb"], 0, 2048, 512,
                   lambda nb, ps, pk: P.op("act", lambda e: e.activation(out=G[4][:, nb, :], in_=ps[:, 0:512], func=AF.Sigmoid),
                                           reads=[pk], writes=[("G", 4, nb)]))

            def ev_b(nb, ps, pk):
                i = tfi[0]
                tfi[0] ^= 1
                tb = tmpb[i]
                P.op("dve", lambda e: e.tensor_tensor(out=tb[:, :], in0=ps[:, 0:512], in1=G[4][:, nb, :], op=ALU.mult),
                     reads=[pk, ("G", 4, nb)], writes=[("tmpb", i)])
                P.op("dve", lambda e: e.tensor_tensor(out=G[3][:, nb, :], in0=G[3][:, nb, :], in1=tb[:, :], op=ALU.add),
                     reads=[("tmpb", i), ("G", 3, nb)], writes=[("G", 3, nb)])
            lin_fm(k, [(G[2][:, 8:16, :], [("G", 2, i) for i in range(8, 16)], 8)], k.Wb["wbb"], 0, 2048, 512, ev_b)
            lin_tm(k, [(G[3], KG(3), 16)], k.Wb["wmix"], 0, 2048, 4, resid_evac)
            layer_norm(0, c, False)
            lin_fm(k, [(G[1], KG(1), 16)], k.Wb["xwq"], 0, 512, 512,
                   lambda nb, ps, pk: evac(k, qT[:, nb, :], ps[:, 0:512], [pk], [("qT", nb)], scale=128.0 ** -0.5))
            for hh in range(4):
                for mb in range(2):
                    S_, sk = bank(k)
                    P.mm(S_[:, 0:512], memKT[:, hh, mb * 128:(mb + 1) * 128], qT[:, hh, :], True, True,
                         reads=[("qT", hh)] + mkv_keys, writes=[sk])
                    pt = PTx[(hh % 2) * 2 + mb]
                    ptk = ("PTx", (hh % 2) * 2 + mb)
                    P.op("act", (lambda pt, S_: lambda e: e.activation(out=pt[:, :], in_=S_[:, 0:512], func=AF.Exp))(pt, S_),
                         reads=[sk], writes=[ptk])
                for tt in range(4):
                    O_, ok_ = bank(k)
                    for mb in range(2):
                        P.mm(O_[:, 0:129], PTx[(hh % 2) * 2 + mb][:, tt * 128:(tt + 1) * 128], memV[:, mb, hh, 0:129], mb == 0, mb == 1,
                             reads=[("PTx", (hh % 2) * 2 + mb)] + mkv_keys, writes=[ok_])
                    P.op("dve", (lambda O_, tt: lambda e: e.reciprocal(out=rd[:, tt:tt + 1], in_=O_[:, 128:129]))(O_, tt),
                         reads=[ok_], writes=[("rd", tt)])
                    P.op("dve", (lambda O_, tt, hh: lambda e: e.tensor_scalar(out=otm[:, tt, hh * 128:(hh + 1) * 128], in0=O_[:, 0:128],
                                                                            scalar1=rd[:, tt:tt + 1], scalar2=None, op0=ALU.mult))(O_, tt, hh),
                         reads=[ok_, ("rd", tt)], writes=[("otm", tt, hh)])
                transpose_block(k, oT[:, hh, :], [otm[:, tt, hh * 128:(hh + 1) * 128] for tt in range(4)],
                                [("otm", tt, hh) for tt in range(4)], ("oT", hh))
            lin_tm(k, [(oT, [("oT", hh) for hh in range(4)], 4)], k.Wb["xwo"], 0, 2048, 4, resid_evac)
            layer_norm(1, c, False)
            hid = [G[0], G[2], G[3], G[4]]
            hidx = [0, 2, 3, 4]

            def ev_h(nb, ps, pk):
                i = tfi[0]
                tfi[0] ^= 1
                tf = tmpf[i]
                dst = hid[nb // 16][:, nb % 16, :]
                P.op("act", lambda e: e.activation(out=tf[:, :], in_=ps[:, 0:512], func=AF.Relu), reads=[pk], writes=[("tmpf", i)])
                P.op("dve", lambda e: e.tensor_tensor(out=dst, in0=tf[:, :], in1=tf[:, :], op=ALU.mult),
                     reads=[("tmpf", i)], writes=[("G", hidx[nb // 16], nb % 16)])
            lin_fm(k, [(G[1], KG(1), 16)], k.Wb["w1"], 0, 8192, 512, ev_h)
            lin_tm(k, [(hid[gi], KG(hidx[gi]), 16) for gi in range(4)], k.Wb["w2"], 0, 2048, 4, resid_evac)
            layer_norm(2, c, True)
        P.flush()


_CONST_CACHE = {}


def make_in_maps(inputs):
    x = np.asarray(inputs["x"], np.float32)
    maps = []
    for c in range(8):
        b, h = c // 2, c % 2
        if h not in _CONST_CACHE:
            _CONST_CACHE[h] = _host_consts(h)
        cst = _CONST_CACHE[h]
        xb = x[b].reshape(NB, 128, D)
        xq = np.ascontiguousarray(xb[h::2].reshape(2048, D))
        if h == 0:
            xf = np.concatenate([np.zeros((128, D), np.float32), x[b, :S - 128]], axis=0)
        else:
            xf = x[b]
        m = {"xq": xq, "xf": np.ascontiguousarray(xf), "mem": np.ascontiguousarray(np.asarray(inputs["mem"], np.float32)[b])}
        for name in ("w_in", "attn_sinks", "cmp_pos_k", "cmp_pos_v", "cmp_w1_k", "cmp_w1_v", "cmp_w2_k", "cmp_w2_v",
                     "w_branch_swa", "w_branch_nsa", "w_mix_out", "ln1_g", "ln1_b", "ln2_g", "ln2_b", "ln3_g", "ln3_b",
                     "xa_w_q", "xa_w_kv", "xa_w_o", "mlp_w1", "mlp_w2"):
            a = np.asarray(inputs[name], np.float32)
            a = a[0]
            if a.ndim == 1:
                a = a[None, :]
            m[name] = np.ascontiguousarray(a)
        m["rel_bias_table"] = np.ascontiguousarray(np.asarray(inputs["rel_bias_table"], np.float32))
        for n in ("swa", "sel", "win", "cmp"):
            m["oh_" + n] = cst["oh_" + n]
        for n in ("memb", "ov", "padb", "cvalid", "smul", "sadd"):
            m[n] = cst[n]
        maps.append(m)
    return maps


_NC_CACHE = {}


def kernel(**inputs):
    if "nc" not in _NC_CACHE:
        _NC_CACHE["nc"] = build()
    nc, k = _NC_CACHE["nc"]
    maps = make_in_maps(inputs)
    res = run_bass_kernel_spmd(nc, maps, core_ids=list(range(8)))
    out = np.zeros((4, S, D), np.float32)
    for c in range(8):
        b, h = c // 2, c % 2
        o = np.asarray(res.results[c]["out"], np.float32).reshape(16, 128, D)
        out[b].reshape(NB, 128, D)[h::2] = o
    return out
```

```python
import math
import os
from contextlib import ExitStack

import numpy as np
import ml_dtypes
import concourse.bass as bass
import concourse.mybir as mybir
from concourse.bass_utils import run_bass_kernel_spmd

F32 = mybir.dt.float32
BF16 = mybir.dt.bfloat16
AF = mybir.ActivationFunctionType
ALU = mybir.AluOpType

D = 2048
S = 4096
NB = 32
NEG = -30000.0
ALPHA = 2.0 ** 0.25
EPS = 1e-5
C_QA, C_KA, C_VA, C_QB, C_KC, C_VC, C_KS, C_VS, C_KW, C_VW, C_GN, C_GA, C_GB = (
    0, 1024, 1152, 1280, 2304, 2432, 2560, 2688, 2816, 2944, 3072, 3120, 5168)
L_SEL = 4224
L_CMP = 6144
L_SWA = 384
L_WIN = 768
OFFC = 2063

DEBUG = {}
UPTO = int(os.environ.get('K_UPTO', '9'))


class _Op:
    __slots__ = ("eng", "fn", "deps", "signal", "count", "dma", "semkey", "semval", "flushed")

    def __init__(self, eng, fn, dma, semkey):
        self.eng = eng
        self.fn = fn
        self.deps = []
        self.signal = False
        self.count = None
        self.dma = dma
        self.semkey = semkey
        self.semval = None
        self.flushed = False


class Prog:
    ENGS = ("pe", "act", "dve", "pool", "sp")

    def __init__(self, nc, stack):
        self.nc = nc
        self.stack = stack
        self.ops = {e: [] for e in self.ENGS}
        self.lastw = {}
        self.readers = {}
        self.semcount = {}
        self.dsem = {}
        self.nblk = 0
        self.pool_dmas = []
        self.esem = None
        self.ebase = None

    def _dep(self, op, other, kind):
        if other is None or other is op or other.flushed:
            return
        if not other.dma and not op.dma and other.eng == op.eng:
            if kind != "RAW" or op.eng == "pe":
                return
        if other not in op.deps:
            op.deps.append(other)
            other.signal = True

    def op(self, eng, fn, reads=(), writes=(), dma=False, semkey=None):
        o = _Op(eng, fn, dma, semkey)
        for r in reads:
            self._dep(o, self.lastw.get(r), "RAW")
        for w in writes:
            self._dep(o, self.lastw.get(w), "WAW")
            for rd in self.readers.get(w, ()):
                self._dep(o, rd, "WAR")
        for r in reads:
            self.readers.setdefault(r, []).append(o)
        for w in writes:
            self.lastw[w] = o
            self.readers[w] = []
        if dma:
            if semkey is None:
                semkey = writes[0]
                o.semkey = semkey
            self.semcount[semkey] = self.semcount.get(semkey, 0) + 16
            o.semval = self.semcount[semkey]
        self.ops[eng].append(o)
        return o

    def dma(self, q, out, in_, reads=(), writes=(), semkey=None, **kw):
        o = self.op(q, lambda e: e.dma_start(out=out, in_=in_, **kw), reads, writes, dma=True, semkey=semkey)
        if q == "pool":
            self.pool_dmas.append(o)
            if len(self.pool_dmas) > 3:
                prev = self.pool_dmas[-4]
                if not prev.flushed and prev not in o.deps:
                    o.deps.append(prev)
        return o

    def mm(self, out, lhsT, rhs, start, stop, reads=(), writes=()):
        return self.op("pe", lambda e: e.matmul(out, lhsT, rhs, start=start, stop=stop), reads, writes)

    def tr(self, out, in_, ident, reads=(), writes=()):
        return self.op("pe", lambda e: e.transpose(out, in_, ident), reads, writes)

    def flush(self):
        nc = self.nc
        self.nblk += 1
        if self.esem is None or max(self.ebase.values()) > 45000:
            self.esem = {e: self.stack.enter_context(nc.semaphore("es%d_%s" % (self.nblk, e))) for e in self.ENGS}
            self.ebase = {e: 0 for e in self.ENGS}
        esem = self.esem
        for k in self.semcount:
            if k not in self.dsem:
                self.dsem[k] = self.stack.enter_context(nc.semaphore("ds%d" % len(self.dsem)))
        dsem = self.dsem
        maxc = {}
        for e in self.ENGS:
            c = self.ebase[e]
            for o in self.ops[e]:
                if o.signal and not o.dma:
                    c += 1
                    o.count = c
            maxc[e] = c if c > self.ebase[e] else 0
            self.ebase[e] = c
        if os.environ.get("K_VERBOSE"):
            print("flush", self.nblk, {e: len(self.ops[e]) for e in self.ENGS}, dict(self.ebase), len(self.dsem))
        final = [(dsem[k], v) for k, v in self.semcount.items()]
        ops = self.ops

        def run(ename, eng, last=False):
            waited = {}
            for o in ops[ename]:
                for d in o.deps:
                    if d.flushed:
                        continue
                    if d.dma:
                        s, v = dsem[d.semkey], d.semval
                    else:
                        s, v = esem[d.eng], d.count
                    key = id(s)
                    if waited.get(key, 0) >= v:
                        continue
                    waited[key] = v
                    eng.wait_ge(s, v)
                ins = o.fn(eng)
                if o.dma:
                    ins.then_inc(dsem[o.semkey], 16)
                elif o.signal:
                    ins.then_inc(esem[ename], 1)
            if last:
                for s, v in final:
                    eng.wait_ge(s, v)
                for e2 in self.ENGS:
                    if e2 != ename and maxc[e2]:
                        eng.wait_ge(esem[e2], maxc[e2])

        with nc.Block() as block:
            @block.tensor
            def _(eng):
                run("pe", eng)

            @block.scalar
            def _(eng):
                run("act", eng)

            @block.vector
            def _(eng):
                run("dve", eng)

            @block.gpsimd
            def _(eng):
                run("pool", eng)

            @block.sync
            def _(eng):
                run("sp", eng, last=True)

        for e in self.ENGS:
            for o in self.ops[e]:
                o.flushed = True
                o.fn = None
                o.deps = None
        self.ops = {e: [] for e in self.ENGS}
        self.lastw = {}
        self.readers = {}


def _bucket(d):
    d = np.maximum(d, 0)
    lr = np.log(np.maximum(d, 1).astype(np.float32) / np.float32(16)) / np.float32(math.log(4096 / 16))
    large = np.minimum(16 + (lr * np.float32(16)).astype(np.int32), 31)
    return np.where(d < 16, d, large)


def _bucket_exact(d):
    import jax
    import jax.numpy as jnp
    with jax.default_device(jax.devices("cpu")[0]):
        dd = jnp.asarray(d, dtype=jnp.int32)
        exact = 16
        dm = jnp.maximum(dd, 0)
        log_ratio = jnp.log(jnp.maximum(dm, 1).astype(jnp.float32) / exact) / math.log(4096 / exact)
        large = jnp.minimum(exact + (log_ratio * 16).astype(jnp.int32), 31)
        return np.asarray(jnp.where(dm < exact, dm, large))


def _onehot_table(L, off, valid_fn):
    i = np.arange(L)
    d = i - off
    valid = valid_fn(d)
    bk = _bucket_exact(d)
    oh = np.zeros((33, L), np.float32)
    oh[bk[valid], i[valid]] = 1.0
    oh[32, i[~valid]] = 1.0
    return oh


def _host_consts(h):
    c = {}
    c["oh_swa"] = _onehot_table(L_SWA, 127, lambda d: (d >= 0) & (d < 128))
    c["oh_sel"] = _onehot_table(L_SEL, 127, lambda d: d >= 0)
    c["oh_win"] = _onehot_table(L_WIN, 127, lambda d: (d >= 0) & (d < 512))
    c["oh_cmp"] = _onehot_table(L_CMP, OFFC, lambda d: d >= 0)
    memb = np.zeros((128, 32, 128), np.float32)
    for l in range(32):
        memb[2 * l, l, 0:64] = 1.0
        memb[2 * l + 1, l, 64:128] = 1.0
    c["memb"] = memb
    ov = np.zeros((128, 2, 65), np.float32)
    for cc in range(255):
        for j in range(64):
            if 16 * cc < 64 * j + 64 and 16 * cc + 32 > 64 * j:
                ov[cc % 128, cc // 128, j] = 1.0
        ov[cc % 128, cc // 128, 64] = 1.0
    c["ov"] = ov
    npad = 1 - h
    c["padb"] = np.full((128, 1), NEG * npad, np.float32)
    cv = np.zeros((128, 1), np.float32)
    if npad:
        cv[0:8, 0] = NEG
    c["cvalid"] = cv
    smul = np.zeros((128, 16, 64), np.float32)
    sadd = np.zeros((128, 16, 64), np.float32)
    q = np.arange(128)[:, None]
    j = np.arange(64)[None, :]
    for t in range(16):
        pos = (2 * t + 1) * 128 + q
        causal = (64 * j <= pos)
        pad = j < 2 * npad
        back = pos // 64 - j
        forced = (j == 2 * npad) | ((back >= 0) & (back < 2))
        ok = causal & ~pad
        smul[:, t, :] = (ok & ~forced).astype(np.float32)
        sadd[:, t, :] = np.where(ok, np.where(forced, 1e4, 0.0), -1.0)
    c["smul"] = smul
    c["sadd"] = sadd
    return c


class K:
    pass


def build(debug=()):
    nc = bass.Bass("TRN2", target_bir_lowering=False)
    k = K()
    k.nc = nc
    k.debug = set(debug)
    k.dbg_out = {}

    def din(name, shape, dt=F32):
        return nc.dram_tensor(name, list(shape), dt, kind="ExternalInput").ap()

    k.xq = din("xq", [2048, D])
    k.xf = din("xf", [S, D])
    k.mem = din("mem", [256, D])
    k.w_in = din("w_in", [D, 7216])
    k.sinks = din("attn_sinks", [1, 16])
    k.rel = din("rel_bias_table", [32, 32])
    k.cpos = {"k": din("cmp_pos_k", [32, 64]), "v": din("cmp_pos_v", [32, 64])}
    k.cw1 = {"k": din("cmp_w1_k", [2048, 256]), "v": din("cmp_w1_v", [2048, 256])}
    k.cw2 = {"k": din("cmp_w2_k", [256, 64]), "v": din("cmp_w2_v", [256, 64])}
    k.wba = din("w_branch_swa", [1024, D])
    k.wbb = din("w_branch_nsa", [1024, D])
    k.wmix = din("w_mix_out", [D, D])
    k.lng = [din("ln%d_g" % i, [1, D]) for i in (1, 2, 3)]
    k.lnb = [din("ln%d_b" % i, [1, D]) for i in (1, 2, 3)]
    k.xwq = din("xa_w_q", [D, 512])
    k.xwkv = din("xa_w_kv", [D, 1024])
    k.xwo = din("xa_w_o", [512, D])
    k.w1 = din("mlp_w1", [D, 8192])
    k.w2 = din("mlp_w2", [8192, D])
    k.oh = {n: din("oh_" + n, [33, L]) for n, L in (("swa", L_SWA), ("sel", L_SEL), ("win", L_WIN), ("cmp", L_CMP))}
    k.memb_d = din("memb", [128, 32, 128])
    k.ov_d = din("ov", [128, 2, 65])
    k.padb_d = din("padb", [128, 1])
    k.cvalid_d = din("cvalid", [128, 1])
    k.smul_d = din("smul", [128, 16, 64])
    k.sadd_d = din("sadd", [128, 16, 64])
    k.out = nc.dram_tensor("out", [2048, D], F32, kind="ExternalOutput").ap()
    k.G = {n: nc.dram_tensor("G_" + n, [32, L], BF16, kind="Internal") for n, L in
           (("swa", L_SWA), ("sel", L_SEL), ("win", L_WIN), ("cmp", L_CMP))}
    k.R = {n: nc.dram_tensor("R_" + n, [16, 16, L], BF16, kind="Internal") for n, L in
           (("swa", L_SWA), ("sel", L_SEL), ("win", L_WIN), ("cmp", L_CMP))}
    k.YT = nc.dram_tensor("YT", [16, 128, 2048], BF16, kind=("ExternalOutput" if "YT" in k.debug else "Internal"))

    with ExitStack() as top:
        k.top = top
        P = Prog(nc, top)
        k.P = P

        def sb(st, name, shape, dt):
            return st.enter_context(nc.sbuf_tensor(name, list(shape), dt))

        k.sb = sb
        k.PSall = top.enter_context(nc.psum_tensor("psall", [128, 8, 512], F32))
        k.PS = [k.PSall[:, i, :] for i in range(8)]
        k.psi = 0
        k.nrot = 8
        k.ws = [sb(top, "ws%d" % i, [128, 16, 512], BF16) for i in range(3)]
        k.wsi = 0
        k.ident = sb(top, "ident", [128, 128], BF16)
        k.identf = sb(top, "identf", [128, 128], F32)
        k.evi = 0

        P.op("pool", lambda e: e.memset(k.identf[:], 0.0), writes=["identf"])
        P.op("pool", lambda e: e.affine_select(out=k.identf[:], in_=k.identf[:], pattern=[[-1, 128]],
                                               compare_op=ALU.not_equal, fill=1.0, base=0, channel_multiplier=1),
             reads=["identf"], writes=["identf"])
        P.op("dve", lambda e: e.tensor_copy(k.ident[:], k.identf[:]), reads=["identf"], writes=["ident"])

        with ExitStack() as sA:
            k.sA = sA
            k.KT = {n: [sb(sA, "KT_%s%d" % (n, g), [128, S], BF16) for g in range(2)] for n in ("swa", "sel", "win")}
            k.V = {n: sb(sA, "V_" + n, [128, NB, 2, 65], BF16) for n in ("swa", "sel", "win")}
            k.kcT = [sb(sA, "kcT%d" % g, [128, 256], BF16) for g in range(2)]
            k.VO = sb(sA, "VO", [128, 2, 2, 129], BF16)
            with ExitStack() as sB:
                k.cmpT = {n: sb(sB, "cmpT_" + n, [128, S], BF16) for n in ("k", "v")}
                with ExitStack() as s1:
                    stage_kv(k, s1)
                    P.flush()
                if UPTO >= 2:
                    with ExitStack() as s1:
                        stage_cmp(k, s1)
                        P.flush()
            if UPTO >= 3:
                stage_attn(k)
        if UPTO >= 4:
            stage_dense(k)
    return nc, k


def bank(k):
    n = k.nrot
    i = k.psi % n
    k.psi = (k.psi + 1) % n
    return k.PS[i], "ps%d" % i


def wslot(k):
    i = k.wsi
    k.wsi = (k.wsi + 1) % 3
    return k.ws[i], "ws%d" % i


def evac(k, out, in_, reads, writes, scale=None, eng=None):
    if eng is None:
        eng = "dve" if (k.evi % 2 == 0) else "act"
        k.evi += 1
    if eng == "dve":
        if scale is None:
            k.P.op("dve", lambda e: e.tensor_copy(out, in_), reads, writes)
        else:
            k.P.op("dve", lambda e: e.tensor_scalar(out=out, in0=in_, scalar1=float(scale), scalar2=None, op0=ALU.mult),
                   reads, writes)
    else:
        if scale is None:
            k.P.op("act", lambda e: e.activation(out=out, in_=in_, func=AF.Copy), reads, writes)
        else:
            k.P.op("act", lambda e: e.activation(out=out, in_=in_, func=AF.Copy, scale=float(scale)), reads, writes)


def transpose_block(k, dst, srcs, reads, wkey, eng=None):
    P = k.P
    ps, pk = bank(k)
    psb = ps[:].bitcast(BF16)
    n = len(srcs)
    for i, s in enumerate(srcs):
        P.tr(psb[:, i * 128:(i + 1) * 128], s, k.ident[:], reads=list(reads) + ["ident"], writes=[pk])
    evac(k, dst, psb[:, 0:n * 128], [pk], wkey if isinstance(wkey, list) else [wkey], eng=eng)


def dbg_dump(k, name, ap_sb, shape, dt, reads):
    if name not in k.debug:
        return
    t = k.nc.dram_tensor("dbg_" + name, list(shape), dt, kind="ExternalOutput").ap()
    k.dbg_out[name] = t
    k.P.dma("sp", t, ap_sb, reads=reads, writes=["dbg_" + name])


def stage_kv(k, st):
    nc, P, sb = k.nc, k.P, k.sb
    WkT = sb(st, "WkT", [128, 16, 8, 128], BF16)
    Wv = sb(st, "Wv", [128, 16, 384], BF16)
    xb = sb(st, "xb1", [128, 2, D], BF16)
    xT = sb(st, "xT1", [128, 16, 256], BF16)
    wv = k.w_in.rearrange("(kc p) c -> p kc c", p=128)
    kblocks = [("swa", 0, C_KA), ("swa", 1, C_KA + 64), ("sel", 0, C_KS), ("sel", 1, C_KS + 64),
               ("win", 0, C_KW), ("win", 1, C_KW + 64)]
    P.dma("pool", xb[:, :, :], k.xf[0:256, :].rearrange("(t p) d -> p t d", p=128), writes=["xb1"])
    stg, stgk = wslot(k)
    stage = stg[:, :, :].rearrange("p a b -> p (a b)")[:, 0:16 * 384].rearrange("p (kc c) -> p kc c", c=384)
    for i, c0 in enumerate((C_KA, C_KS, C_KW)):
        P.dma("pool", stage[:, :, i * 128:(i + 1) * 128], wv[:, :, c0:c0 + 128], writes=[("stg", i)])
    wkeys = []
    for bi, (n_, g_, _) in enumerate(kblocks):
        i = bi // 2
        for hh in range(2):
            key = ("WkT", bi, hh)
            wkeys.append(key)
            eng_ = "dve" if hh == 0 else "pool"
            P.op(eng_, (lambda bi, hh, i, g_: lambda e: e.tensor_copy(WkT[:, :, bi, hh * 64:(hh + 1) * 64], stage[:, :, i * 128 + g_ * 64:i * 128 + g_ * 64 + 64]))(bi, hh, i, g_),
                 reads=[("stg", i)], writes=[key])
    for bi, c0 in ((6, C_KC), (7, C_VC)):
        key = ("WkT", bi, 0)
        wkeys.append(key)
        P.dma("pool", WkT[:, :, bi, :], wv[:, :, c0:c0 + 128], writes=[key])
    vkeys = []
    for i, c0 in enumerate((C_VA, C_VS, C_VW)):
        key = ("Wv", i)
        vkeys.append(key)
        P.dma("pool", Wv[:, :, i * 128:(i + 1) * 128], wv[:, :, c0:c0 + 128], writes=[key])
    for n in ("swa", "sel", "win"):
        P.op("pool", (lambda n: lambda e: e.memset(k.V[n][:, :, :, 64:65], 1.0))(n), writes=[("V1", n)])
    tab = sb(st, "tab", [33, 32], BF16)
    P.dma("pool", tab[0:32, :], k.rel, writes=[("tab", 0)])
    P.op("dve", lambda e: e.memset(tab[32:33, :], NEG), writes=[("tab", 1)])
    ohs = sb(st, "ohs", [33, 1024], BF16)
    gsb = sb(st, "gsb", [32, 1024], BF16)

    def table_work():
        for n, L in (("cmp", L_CMP), ("sel", L_SEL), ("win", L_WIN), ("swa", L_SWA)):
            Gt = k.G[n]
            for c0 in range(0, L, 1024):
                w = min(1024, L - c0)
                P.dma("pool", ohs[:, 0:w], k.oh[n][:, c0:c0 + w], writes=["ohs"], max_dma_last_dim=4096)
                for s0 in range(0, w, 512):
                    ww = min(512, w - s0)
                    ps, pk = bank(k)
                    P.mm(ps[0:32, 0:ww], tab[:, :], ohs[:, s0:s0 + ww], True, True, reads=["ohs", ("tab", 0), ("tab", 1)], writes=[pk])
                    evac(k, gsb[:, s0:s0 + ww], ps[0:32, 0:ww], [pk], [("gsb", s0)], eng="dve")
                P.dma("sp", Gt.ap()[:, c0:c0 + w], gsb[:, 0:w], reads=[("gsb", 0), ("gsb", 512)], writes=[("G", n, c0)], semkey=("Gw", n, c0 % 2048))
                yield
            h0 = 0 if n == "swa" else 16
            for hh in range(16):
                src = bass.AP(Gt, (h0 + hh) * L, [[0, 16], [1, L]])
                P.dma("sp", k.R[n].ap()[hh], src, reads=[("G", n, c0) for c0 in range(0, L, 1024)], writes=[("R", n, hh)], semkey="Rtab")
        while True:
            yield
    tw = table_work()
    for c in range(int(os.environ.get('K_NCH', '16'))):
        t0 = c * 256
        next(tw)
        if c > 0:
            P.dma("pool", xb[:, :, :], k.xf[t0:t0 + 256, :].rearrange("(t p) d -> p t d", p=128), writes=["xb1"])
        MODE = os.environ.get('K_MODE', 'ABCD')
        for kp in range(8 if 'B' in MODE else 0):
            srcs = [xb[:, tt, (2 * kp + kk) * 128:(2 * kp + kk + 1) * 128] for kk in range(2) for tt in range(2)]
            transpose_block(k, xT[:, 2 * kp:2 * kp + 2, :], srcs, ["xb1"], ("xT1", kp))
        xkeys = [("xT1", kp) for kp in range(8)]
        for bi in range(8 if 'C' in MODE else 0):
            ps, pk = bank(k)
            for kc in range(16):
                P.mm(ps[:, 0:256], WkT[:, kc, bi, :], xT[:, kc, :], kc == 0, kc == 15,
                     reads=[("xT1", kc // 2)] + (wkeys if kc == 0 else []), writes=[pk])
            if bi < 6:
                n, g, _ = kblocks[bi]
                dst, dk = k.KT[n][g][:, t0:t0 + 256], ("KT", n, g, c)
            else:
                n = "k" if bi == 6 else "v"
                dst, dk = k.cmpT[n][:, t0:t0 + 256], ("cmpT", n, c)
            evac(k, dst, ps[:, 0:256], [pk], [dk])
        for tt in range(2 if 'D' in MODE else 0):
            ps, pk = bank(k)
            for kc in range(16):
                P.mm(ps[:, 0:384], xT[:, kc, tt * 128:(tt + 1) * 128], Wv[:, kc, :], kc == 0, kc == 15,
                     reads=[("xT1", kc // 2)] + (vkeys if kc == 0 else []), writes=[pk])
            blk = 2 * c + tt
            for i, n in enumerate(("swa", "sel", "win")):
                evac(k, k.V[n][:, blk, :, 0:64],
                     ps[:, i * 128:(i + 1) * 128].rearrange("p (g d) -> p g d", g=2), [pk], [("V", n, blk)],
                     eng=os.environ.get("K_VENG", "dve"))
    if "KT_sel0" in k.debug:
        dbg_dump(k, "KT_sel0", k.KT["sel"][0][:], [128, S], BF16, [("KT", "sel", 0, c) for c in range(16)])
    if "V_win" in k.debug:
        dbg_dump(k, "V_win", k.V["win"][:], [128, NB, 2, 65], BF16,
                 [("V", "win", b) for b in range(NB)] + [("V1", "win")])
    if "cmpT_k" in k.debug:
        dbg_dump(k, "cmpT_k", k.cmpT["k"][:], [128, S], BF16, [("cmpT", "k", c) for c in range(16)])


def stage_cmp(k, st):
    nc, P, sb = k.nc, k.P, k.sb
    cmp_reads = {n: [("cmpT", n, c) for c in range(16)] for n in ("k", "v")}
    w1 = {n: sb(st, "w1" + n, [128, 32, 256], BF16) for n in ("k", "v")}
    w2k = sb(st, "w2k", [128, 2, 128], BF16)
    w2v = sb(st, "w2v", [128, 2, 64], BF16)
    posT = {n: sb(st, "posT" + n, [128, 32], BF16) for n in ("k", "v")}
    hb = sb(st, "hbias", [128, 4], F32)
    hT = {(n, g): sb(st, "hT%s%d" % (n, g), [128, 2, 256], BF16) for n in ("k", "v") for g in range(2)}
    xs = sb(st, "gx", [128, 256], F32)
    u = sb(st, "gu", [128, 256], F32)
    sg = sb(st, "gs", [128, 256], F32)
    for n in ("k", "v"):
        src = k.cw1[n].rearrange("(l d) h -> d l h", d=64)
        for hh in range(2):
            P.dma("pool", w1[n][hh * 64:(hh + 1) * 64, :, :], src, writes=[("w1", n, hh)])
        for hh in range(2):
            P.dma("pool", posT[n][hh * 64:(hh + 1) * 64, :], k.cpos[n].rearrange("l d -> d l"),
                  writes=[("posT", n, hh)], allow_slow_non_contiguous=True)
    for hh in range(2):
        P.dma("pool", w2k[:, :, hh * 64:(hh + 1) * 64], k.cw2["k"].rearrange("(c p) d -> p c d", p=128),
              writes=[("w2k", hh)])
    P.dma("pool", w2v[:, :, :], k.cw2["v"].rearrange("(c p) d -> p c d", p=128), writes=[("w2v",)])
    P.dma("pool", k.VO[:, :, 0, 64:129], k.ov_d, writes=[("VOc", 0)])
    P.dma("pool", k.VO[:, :, 1, 64:129], k.ov_d, writes=[("VOc", 1)])
    for key in hT:
        P.op("pool", (lambda t: lambda e: e.memset(t[:], 0.0))(hT[key]), writes=[("hT", key)])
    for ni, n in enumerate(("k", "v")):
        for hc in range(2):
            ps, pk = bank(k)
            for l in range(32):
                P.mm(ps[:, 0:1], w1[n][0:64, l, hc * 128:(hc + 1) * 128], posT[n][0:64, l:l + 1], l == 0, l == 31,
                     reads=[("w1", n, 0), ("posT", n, 0)], writes=[pk])
            evac(k, hb[:, 2 * ni + hc:2 * ni + hc + 1], ps[:, 0:1], [pk], [("hb", ni, hc)], eng="dve")
    for ni, n in enumerate(("k", "v")):
        for g in range(2):
            for hc in range(2):
                ps, pk = bank(k)
                for l in range(32):
                    P.mm(ps[:, 0:255], w1[n][g * 64:(g + 1) * 64, l, hc * 128:(hc + 1) * 128],
                         k.cmpT[n][g * 64:(g + 1) * 64, l:l + 16 * 254 + 1:16], l == 0, l == 31,
                         reads=[("w1", n, g)] + (cmp_reads[n] if l == 0 else []), writes=[pk])
                bcol = hb[:, 2 * ni + hc:2 * ni + hc + 1]
                X, U, SG = xs[:, 0:255], u[:, 0:255], sg[:, 0:255]
                P.op("dve", (lambda ps, bcol, X: lambda e: e.tensor_scalar(out=X, in0=ps[:, 0:255], scalar1=bcol, scalar2=None, op0=ALU.add))(ps, bcol, X),
                     reads=[pk, ("hb", ni, hc)], writes=["gx"])
                P.op("dve", (lambda X, U: lambda e: e.tensor_tensor(out=U, in0=X, in1=X, op=ALU.mult))(X, U), reads=["gx"], writes=["gu"])
                P.op("dve", (lambda U: lambda e: e.tensor_scalar(out=U, in0=U, scalar1=0.044715, scalar2=1.0, op0=ALU.mult, op1=ALU.add))(U),
                     reads=["gu"], writes=["gu"])
                P.op("dve", (lambda X, U: lambda e: e.tensor_tensor(out=U, in0=U, in1=X, op=ALU.mult))(X, U), reads=["gu", "gx"], writes=["gu"])
                P.op("act", (lambda U, SG: lambda e: e.activation(out=SG, in_=U, func=AF.Sigmoid, scale=1.5957691216057308))(U, SG),
                     reads=["gu"], writes=["gs"])
                dst = hT[(n, g)][:, hc, 0:255]
                P.op("dve", (lambda X, SG, dst: lambda e: e.tensor_tensor(out=dst, in0=X, in1=SG, op=ALU.mult))(X, SG, dst),
                     reads=["gx", "gs", ("hT", (n, g))], writes=[("hT", (n, g), hc)])
    for g in range(2):
        hkeys = [("hT", ("k", g), hc) for hc in range(2)] + [("hT", ("k", g))]
        ps, pk = bank(k)
        for hc in range(2):
            P.mm(ps[:, 0:256], w2k[:, hc, :], hT[("k", g)][:, hc, :], hc == 0, hc == 1,
                 reads=hkeys + [("w2k", 0), ("w2k", 1)], writes=[pk])
        evac(k, k.kcT[g][:, :], ps[:, 0:256], [pk], [("kcT", g)], eng="dve")
        hkeys = [("hT", ("v", g), hc) for hc in range(2)] + [("hT", ("v", g))]
        for cb in range(2):
            ps, pk = bank(k)
            for hc in range(2):
                P.mm(ps[:, 0:64], hT[("v", g)][:, hc, cb * 128:(cb + 1) * 128], w2v[:, hc, :], hc == 0, hc == 1,
                     reads=hkeys + [("w2v",)], writes=[pk])
            evac(k, k.VO[:, cb, g, 0:64], ps[:, 0:64], [pk], [("VOv", cb, g)], eng="dve")
    if "kcT0" in k.debug:
        dbg_dump(k, "kcT0", k.kcT[0][:], [128, 256], BF16, [("kcT", 0)])
    if "VO" in k.debug:
        dbg_dump(k, "VO", k.VO[:], [128, 2, 2, 129], BF16,
                 [("VOv", cb, g) for cb in range(2) for g in range(2)] + [("VOc", 0), ("VOc", 1)])


def stage_attn(k):
    nc, P, sb = k.nc, k.P, k.sb
    k.nrot = 4
    k.psi = 0
    OB = [(k.PS[4 + s_], "ps%d" % (4 + s_)) for s_ in range(4)]
    oset = [0]
    with ExitStack() as st:
        Wswa = sb(st, "Wswa", [128, 16, 256], BF16)
        Wwin4 = sb(st, "Wwin4", [128, 16, 128], BF16)
        memb = sb(st, "membs", [128, 32, 128], BF16)
        padb = sb(st, "padbs", [128, 1], F32)
        cvalid = sb(st, "cvalids", [128, 1], F32)
        sinkexp = sb(st, "sinkexp", [128, 16], F32)
        Wg = sb(st, "Wg", [128, 16, 48], BF16)
        xb = sb(st, "xb2", [128, 2, D], BF16)
        xTq = sb(st, "xTq", [128, 16, 256], BF16)
        QT = sb(st, "QT", [128, 16, 2, 256], BF16)
        gate = sb(st, "gate", [128, 2, 48], F32)
        ya = sb(st, "ya", [128, 2, 1024], BF16)
        yb = sb(st, "yb", [128, 2, 1024], BF16)
        Wc = [sb(st, "Wc%d" % i, [128, 16, 128], BF16) for i in range(2)]
        PT = [sb(st, "PT%d" % i, [128, 2, 256], BF16) for i in range(4)]
        score = [sb(st, "score%d" % g, [128, 64], F32) for g in range(2)]
        smul = sb(st, "smuls", [128, 2, 64], F32)
        sadd = sb(st, "sadds", [128, 2, 64], F32)
        s1 = sb(st, "s1", [128, 64], F32)
        wk = sb(st, "wk", [128, 64], F32)
        m8 = sb(st, "m8", [128, 8], F32)
        selm = sb(st, "selm", [128, 64], F32)
        negsel = sb(st, "negsel", [128, 64], BF16)
        negselT = sb(st, "negselT", [128, 2, 2, 128], BF16)
        negsel4 = sb(st, "negsel4", [128, 2, 2, 64], BF16)
        dn = sb(st, "dn", [128, 8], F32)
        dn2 = sb(st, "dn2", [128, 16], F32)
        pti = [0]
        dni = [0]
        wc_keys = {}

        Lw, Ls, Lc, La = L_WIN, L_SEL, L_CMP, L_SWA
        def toeplitz(dst, rname, L, base, pstep, nh, h0, ncols, wkey, plain=None):
            for a in range(8):
                src = bass.AP(k.R[rname], h0 * 16 * L + base - pstep * 16 * a, [[L - pstep, 16], [16 * L, nh], [1, ncols]])
                P.dma("sp", dst[16 * a:16 * a + 16, :, 0:ncols], src,
                      writes=[(wkey, a)] + ([plain] if (plain is not None and a == 0) else []), semkey=(wkey, a % 2))
            return [(wkey, a) for a in range(8)] + ([plain] if plain is not None else [])

        swa_keys = toeplitz(Wswa, "swa", La, 127, 1, 16, 0, 256, "Wswa")
        win_keys = toeplitz(Wwin4, "win", Lw, 127 + 512, 1, 16, 0, 128, "Wwin4")
        P.dma("pool", memb[:, :, :], k.memb_d, writes=["memb"], max_dma_last_dim=2048)
        P.op("dve", lambda e: e.memset(QT[:, :, :, :], 0.0), writes=["QTz"])
        P.op("dve", lambda e: e.memset(negselT[:, :, :, :], 0.0), writes=["nsz"])
        P.dma("sp", padb[:, :], k.padb_d, writes=["padb"])
        P.dma("sp", cvalid[:, :], k.cvalid_d, writes=["cvalid"])
        P.dma("sp", sinkexp[:, :], k.sinks.broadcast_to([128, 16]), writes=["sinkexp"])
        P.op("act", lambda e: e.activation(out=sinkexp[:, :], in_=sinkexp[:, :], func=AF.Exp), reads=["sinkexp"], writes=["sinkexp"])
        P.dma("pool", Wg[:, :, :], k.w_in.rearrange("(kc p) c -> p kc c", p=128)[:, :, C_GN:C_GN + 48], writes=["Wg"])
        wv = k.w_in.rearrange("(kc p) c -> p kc c", p=128)

        pend = []

        def run_pending(keep=0):
            while len(pend) > keep:
                pend.pop(0)()

        spi = [0]

        def attn_unit(kind, tt, t, j, qblk, g, lset, KTt, Vt, ncol_o, dcol, wfn, bias_fn, mask, fin):
            ob = OB[oset[0]]
            oset[0] = (oset[0] + 1) % 4
            nl = len(lset)
            groups = []
            i = 0
            while i < nl:
                if (kind != "cmp" and i + 1 < nl and bias_fn(lset[i]) is None and bias_fn(lset[i + 1]) is None):
                    groups.append([i, i + 1])
                    i += 2
                else:
                    groups.append([i])
                    i += 1
            for grp in groups:
                if len(grp) == 2:
                    p_ = spi[0]
                    spi[0] ^= 1
                    Sb = [(k.PS[2 * p_], "ps%d" % (2 * p_)), (k.PS[2 * p_ + 1], "ps%d" % (2 * p_ + 1))]
                    S_in = k.PSall[:, 2 * p_:2 * p_ + 2, 0:256]
                else:
                    Sb = [bank(k)]
                    S_in = Sb[0][0][:, 0:256]
                sks = [b[1] for b in Sb]
                for gi_, li in enumerate(grp):
                    l = lset[li]
                    S_, sk = Sb[gi_]
                    for hh in range(2):
                        P.mm(S_[:, hh * 128:(hh + 1) * 128], KTt[:, l * 128:(l + 1) * 128], QT[:, qblk, hh, tt * 128:(tt + 1) * 128], hh == 0, False,
                             reads=[("QT", qblk), "QTz"], writes=[sk])
                    wt, wkeys = wfn(l)
                    P.mm(S_[:, 0:256].rearrange("p (h q) -> p h q", h=2), k.ident[:, :], wt, False, mask is None,
                         reads=wkeys, writes=[sk])
                    if mask is not None:
                        for hh in range(2):
                            P.mm(S_[:, hh * 128:(hh + 1) * 128], memb[:, l, :], negselT[:, g, tt, :], False, hh == 1,
                                 reads=["memb", ("negselT", g, tt), "nsz"], writes=[sk])
                pt = PT[pti[0]]
                ptk = "PT%d" % pti[0]
                pti[0] = (pti[0] + 1) % len(PT)
                ng = len(grp)
                pt_out = pt[:, 0:ng, :] if ng == 2 else pt[:, 0, :]
                bap = bias_fn(lset[grp[0]]) if ng == 1 else None
                if bap is None:
                    P.op("act", (lambda pt_out, S_in: lambda e: e.activation(out=pt_out, in_=S_in, func=AF.Exp))(pt_out, S_in),
                         reads=sks, writes=[ptk])
                else:
                    P.op("act", (lambda pt_out, S_in, bap: lambda e: e.activation(out=pt_out, in_=S_in, func=AF.Exp, bias=bap[0]))(pt_out, S_in, bap),
                         reads=sks + [bap[1]], writes=[ptk])
                run_pending(1)

                def pv(grp=grp, pt=pt, ptk=ptk):
                    for gi_, li in enumerate(grp):
                        l = lset[li]
                        for hh in range(2):
                            P.mm(ob[0][:, hh * ncol_o:(hh + 1) * ncol_o], pt[:, gi_, hh * 128:(hh + 1) * 128], Vt[:, l, g, 0:ncol_o],
                                 li == 0 and hh == 0, li == nl - 1 and hh == 1, reads=[ptk], writes=[ob[1]])
                    if grp[-1] == nl - 1:
                        fin(ob)
                pend.append(pv)

        def small2():
            i = dni[0]
            dni[0] = (dni[0] + 1) % 8
            return dn2[:, 2 * i:2 * i + 2], ("dn2", i)

        def small():
            i = dni[0]
            dni[0] = (dni[0] + 1) % 8
            return dn[:, i:i + 1], ("dn", i)

        for c in range(int(os.environ.get("K_NCHA", "8"))):
            tok0 = c * 256
            P.dma("pool", xb[:, :, :], k.xq[tok0:tok0 + 256, :].rearrange("(t p) d -> p t d", p=128), writes=["xb2"])
            P.dma("sp", smul[:, :, :], k.smul_d[:, 2 * c:2 * c + 2, :], writes=["smul"])
            P.dma("sp", sadd[:, :, :], k.sadd_d[:, 2 * c:2 * c + 2, :], writes=["sadd"])
            for kp in range(8):
                srcs = [xb[:, tt, (2 * kp + kk) * 128:(2 * kp + kk + 1) * 128] for kk in range(2) for tt in range(2)]
                transpose_block(k, xTq[:, 2 * kp:2 * kp + 2, :], srcs, ["xb2"], ("xTq", kp))
            xkeys = [("xTq", kp) for kp in range(8)]
            AT = os.environ.get('K_AT', 'qgscny')
            for wb, c0 in enumerate((C_QA, C_QA + 512, C_QB, C_QB + 512) if 'q' in AT else ()):
                wt, wkey = wslot(k)
                P.dma("pool", wt[:, :, :], wv[:, :, c0:c0 + 512], writes=[wkey])
                for sbk in range(4):
                    ps, pk = bank(k)
                    for kc in range(16):
                        P.mm(ps[:, 0:256], wt[:, kc, sbk * 128:(sbk + 1) * 128], xTq[:, kc, :], kc == 0, kc == 15,
                             reads=[("xTq", kc // 2), wkey], writes=[pk])
                    qb = wb * 4 + sbk
                    evac(k, QT[0:64, qb, 0, :], ps[0:64, 0:256], [pk, "QTz"], [("QT", qb)], scale=0.125)
                    evac(k, QT[64:128, qb, 1, :], ps[64:128, 0:256], [pk, "QTz"], [("QT", qb)], scale=0.125)
            for tt in range(2 if 'g' in AT else 0):
                ps, pk = bank(k)
                for kc in range(16):
                    P.mm(ps[:, 0:48], xTq[:, kc, tt * 128:(tt + 1) * 128], Wg[:, kc, :], kc == 0, kc == 15,
                         reads=[("xTq", kc // 2), "Wg"], writes=[pk])
                P.op("act", (lambda ps, tt: lambda e: e.activation(out=gate[:, tt, :], in_=ps[:, 0:48], func=AF.Sigmoid))(ps, tt),
                     reads=[pk], writes=[("gate", tt)])
            for tt in range(2 if 'c' in AT else 0):
                t = 2 * c + tt
                lq = 2 * t + 1
                cbs = [0] if lq < 16 else [0, 1]
                for cb in cbs:
                    base = OFFC - 31 + 128 * lq - 2048 * cb
                    wc_keys[cb] = toeplitz(Wc[cb], "cmp", Lc, base, 16, 16, 0, 128, ("Wc", cb))
                for j in range(8):
                    g = j // 4
                    def fin_cmp(ob, j=j, tt=tt, g=g):
                        O_, ok_ = ob
                        O3 = O_[:, 0:258].rearrange("p (h c) -> p h c", h=2)
                        d1, dk1 = small2()
                        d2, dk2 = small2()
                        P.op("dve", lambda e: e.tensor_scalar(out=d1, in0=O3[:, :, 128], scalar1=1e-30, scalar2=None, op0=ALU.max), reads=[ok_], writes=[dk1])
                        P.op("dve", lambda e: e.reciprocal(out=d1, in_=d1), reads=[dk1], writes=[dk1])
                        P.op("dve", lambda e: e.tensor_tensor(out=d2, in0=d1, in1=gate[:, tt, 6 * j:6 * j + 4:3], op=ALU.mult),
                             reads=[dk1, ("gate", tt)], writes=[dk2])
                        for hh in range(2):
                            hd = 2 * j + hh
                            P.op("dve", (lambda hd, hh: lambda e: e.tensor_scalar(out=yb[:, tt, hd * 64:(hd + 1) * 64], in0=O3[:, hh, 0:64], scalar1=d2[:, hh:hh + 1], scalar2=None, op0=ALU.mult))(hd, hh),
                                 reads=[ok_, dk2], writes=[("yb", tt, hd)])
                            if j % 4 == 0 and hh == 0:
                                P.op("dve", (lambda hh: lambda e: e.tensor_scalar(out=score[g][:, :], in0=O3[:, hh, 64:128], scalar1=d1[:, hh:hh + 1], scalar2=None, op0=ALU.mult))(hh),
                                     reads=[ok_, dk1], writes=[("score", g)])
                            else:
                                P.op("dve", (lambda hh: lambda e: e.scalar_tensor_tensor(out=score[g][:, :], in0=O3[:, hh, 64:128], scalar=d1[:, hh:hh + 1], in1=score[g][:, :], op0=ALU.mult, op1=ALU.add))(hh),
                                     reads=[ok_, dk1, ("score", g)], writes=[("score", g)])
                    attn_unit("cmp", tt, t, j, 8 + j, g, cbs, k.kcT[g], k.VO, 129, 128,
                              lambda l, j=j: (Wc[l][:, 2 * j:2 * j + 2, :], wc_keys[l]),
                              lambda l: ((cvalid[:, 0:1], "cvalid") if l == 0 else None), None, fin_cmp)
                run_pending()
                for g in range(2):
                    P.op("dve", (lambda g, tt: lambda e: e.tensor_tensor(out=s1[:, :], in0=score[g][:, :], in1=smul[:, tt, :], op=ALU.mult))(g, tt),
                         reads=[("score", g), "smul"], writes=["s1"])
                    P.op("dve", (lambda tt: lambda e: e.tensor_tensor(out=s1[:, :], in0=s1[:, :], in1=sadd[:, tt, :], op=ALU.add))(tt),
                         reads=["s1", "sadd"], writes=["s1"])
                    P.op("dve", lambda e: e.max(out=m8[:, :], in_=s1[:, :]), reads=["s1"], writes=["m8"])
                    P.op("dve", lambda e: e.match_replace(out=wk[:, :], in_to_replace=m8[:, :], in_values=s1[:, :], imm_value=-1e9),
                         reads=["s1", "m8"], writes=["wk"])
                    P.op("dve", lambda e: e.max(out=m8[:, :], in_=wk[:, :]), reads=["wk"], writes=["m8"])
                    P.op("dve", lambda e: e.tensor_scalar(out=selm[:, :], in0=s1[:, :], scalar1=m8[:, 7:8], scalar2=None, op0=ALU.is_ge),
                         reads=["s1", "m8"], writes=["selm"])
                    P.op("dve", lambda e: e.scalar_tensor_tensor(out=selm[:, :], in0=s1[:, :], scalar=0.0, in1=selm[:, :], op0=ALU.is_ge, op1=ALU.mult),
                         reads=["s1", "selm"], writes=["selm"])
                    P.op("dve", lambda e: e.tensor_scalar(out=negsel[:, :], in0=selm[:, :], scalar1=-1.0, scalar2=-NEG, op0=ALU.add, op1=ALU.mult),
                         reads=["selm"], writes=["negsel"])
                    if "selm" in k.debug and c == int(os.environ.get("K_DBGC", "0")):
                        dbg_dump(k, "selm_%d_%d" % (tt, g), selm[:, :], [128, 64], F32, ["selm"])
                        k.debug.add("selm_%d_%d" % (tt, g))
                        dbg_dump(k, "selm_%d_%d" % (tt, g), selm[:, :], [128, 64], F32, ["selm"])
                    P.op("dve", (lambda g, tt: lambda e: e.tensor_copy(negsel4[:, tt, g, :], negsel[:, :]))(g, tt), reads=["negsel"], writes=[("negsel4", tt, g)])
            for tt in range(2 if 's' in AT else 0):
                t = 2 * c + tt
                lq = 2 * t + 1
                for j in range(8):
                    g = j // 4
                    def fin_swa(ob, j=j, tt=tt):
                        O_, ok_ = ob
                        O3 = O_[:, 0:130].rearrange("p (h c) -> p h c", h=2)
                        d1, dk1 = small2()
                        P.op("dve", lambda e: e.tensor_tensor(out=d1, in0=O3[:, :, 64], in1=sinkexp[:, 2 * j:2 * j + 2], op=ALU.add),
                             reads=[ok_, "sinkexp"], writes=[dk1])
                        P.op("dve", lambda e: e.reciprocal(out=d1, in_=d1), reads=[dk1], writes=[dk1])
                        for hh in range(2):
                            hd = 2 * j + hh
                            P.op("dve", (lambda hd, hh: lambda e: e.tensor_scalar(out=ya[:, tt, hd * 64:(hd + 1) * 64], in0=O3[:, hh, 0:64], scalar1=d1[:, hh:hh + 1], scalar2=None, op0=ALU.mult))(hd, hh),
                                 reads=[ok_, dk1], writes=[("ya", tt, hd)])
                    attn_unit("swa", tt, t, j, j, g, [lq - 1, lq], k.KT["swa"][g], k.V["swa"], 65, 64,
                              lambda l, j=j, lq=lq: (Wswa[:, 2 * j:2 * j + 2, (lq - l) * 128:(lq - l + 1) * 128], swa_keys),
                              lambda l: ((padb[:, 0:1], "padb") if l == 0 else None), None, fin_swa)
            run_pending()
            for tt in range(2 if 'c' in AT else 0):
                for g in range(2):
                    ps, pk = bank(k)
                    psb = ps[:].bitcast(BF16)
                    P.tr(psb[0:64, 0:128], negsel4[:, tt, g, :], k.ident[:, :], reads=[("negsel4", tt, g), "ident"], writes=[pk])
                    evac(k, negselT[0:64, g, tt, :], psb[0:64, 0:128], [pk, "nsz"], [("negselT", g, tt)], eng="dve")
            lqmax = 2 * (2 * c + 1) + 1
            ncols = 128 * (lqmax + 1)
            for j in range(8 if 'n' in AT else 0):
                g = j // 4
                wt, wkey = wslot(k)
                wsel = wt[:, :, :].rearrange("p a b -> p (a b)").rearrange("p (h u) -> p h u", h=2)
                wkeys_sel = toeplitz(wsel, "sel", Ls, 127, 1, 2, 2 * j, ncols, ("T", wkey), plain=wkey)
                for tt in range(2):
                    t = 2 * c + tt
                    lq = 2 * t + 1
                    for br, (kind, lset) in enumerate((("sel", list(range(0, lq + 1))), ("win", list(range(max(0, lq - 4), lq + 1))))):
                        if kind == "sel":
                            wfn = lambda l, lq=lq, wsel=wsel, wkey=wkeys_sel: (wsel[:, :, (lq - l) * 128:(lq - l + 1) * 128], wkey)
                            bfn = lambda l: None
                            mask = True
                        else:
                            wfn = (lambda l, lq=lq, wsel=wsel, wkey=wkeys_sel, j=j:
                                   ((wsel[:, :, (lq - l) * 128:(lq - l + 1) * 128], wkey) if lq - l < 4
                                    else (Wwin4[:, 2 * j:2 * j + 2, :], win_keys)))
                            bfn = lambda l: ((padb[:, 0:1], "padb") if l == 0 else None)
                            mask = None
                        def fin_sw(ob, j=j, tt=tt, br=br):
                            O_, ok_ = ob
                            O3 = O_[:, 0:130].rearrange("p (h c) -> p h c", h=2)
                            d1, dk1 = small2()
                            P.op("dve", lambda e: e.reciprocal(out=d1, in_=O3[:, :, 64]), reads=[ok_], writes=[dk1])
                            P.op("dve", lambda e: e.tensor_tensor(out=d1, in0=d1, in1=gate[:, tt, 6 * j + 1 + br:6 * j + 5 + br:3], op=ALU.mult),
                                 reads=[dk1, ("gate", tt)], writes=[dk1])
                            for hh in range(2):
                                hd = 2 * j + hh
                                P.op("dve", (lambda hd, hh: lambda e: e.scalar_tensor_tensor(out=yb[:, tt, hd * 64:(hd + 1) * 64], in0=O3[:, hh, 0:64], scalar=d1[:, hh:hh + 1], in1=yb[:, tt, hd * 64:(hd + 1) * 64], op0=ALU.mult, op1=ALU.add))(hd, hh),
                                     reads=[ok_, dk1, ("yb", tt, hd)], writes=[("yb", tt, hd)])
                        attn_unit(kind, tt, t, j, 8 + j, g, lset, k.KT[kind][g], k.V[kind], 65, 64, wfn, bfn, mask, fin_sw)
            run_pending()
            for bp in range(8 if 'y' in AT else 0):
                src_t = ya if bp < 4 else yb
                b0 = (bp % 4) * 2
                srcs = [src_t[:, tt, (b0 + kk) * 128:(b0 + kk + 1) * 128] for kk in range(2) for tt in range(2)]
                rk = [("ya" if bp < 4 else "yb", tt, hd) for tt in range(2) for hd in range(2 * b0, 2 * b0 + 4)]
                transpose_block(k, xTq[:, 2 * bp:2 * bp + 2, :], srcs, rk, ("xTq", bp))
            P.dma("sp", k.YT.ap().rearrange("b p t -> p b t")[:, :, tok0:tok0 + 256], xTq[:, :, :],
                  reads=[("xTq", bp) for bp in range(8)], writes=["YT"])
            if c == 7:
                P.flush()
    k.nrot = 8
    k.psi = 0


def lin_fm(k, srcs, Wd, c0, ncols, T, evac_fn):
    P = k.P
    Wv = Wd.rearrange("(kc p) c -> p kc c", p=128)
    ng = len(srcs)
    for cb in range(ncols // 512):
        banks = [bank(k) for _ in range(4)]
        kc0 = 0
        for gi, (src, skeys, nkc) in enumerate(srcs):
            wt, wkey = wslot(k)
            P.dma("pool", wt[:, 0:nkc, :], Wv[:, kc0:kc0 + nkc, c0 + cb * 512:c0 + (cb + 1) * 512], writes=[wkey])
            for sbk in range(4):
                ps, pk = banks[sbk]
                for kc in range(nkc):
                    P.mm(ps[:, 0:T], wt[:, kc, sbk * 128:(sbk + 1) * 128], src[:, kc, 0:T],
                         gi == 0 and kc == 0, gi == ng - 1 and kc == nkc - 1,
                         reads=[wkey] + (skeys if kc == 0 else []), writes=[pk])
            kc0 += nkc
        for sbk in range(4):
            evac_fn(cb * 4 + sbk, banks[sbk][0], banks[sbk][1])


def lin_tm(k, srcs, Wd, c0, ncols, ntt, evac_fn):
    P = k.P
    Wv = Wd.rearrange("(kc p) c -> p kc c", p=128)
    ng = len(srcs)
    for cb in range(ncols // 512):
        banks = [bank(k) for _ in range(ntt)]
        kc0 = 0
        for gi, (src, skeys, nkc) in enumerate(srcs):
            wt, wkey = wslot(k)
            P.dma("pool", wt[:, 0:nkc, :], Wv[:, kc0:kc0 + nkc, c0 + cb * 512:c0 + (cb + 1) * 512], writes=[wkey])
            for tt in range(ntt):
                ps, pk = banks[tt]
                for kc in range(nkc):
                    P.mm(ps[:, 0:512], src[:, kc, tt * 128:(tt + 1) * 128], wt[:, kc, :],
                         gi == 0 and kc == 0, gi == ng - 1 and kc == nkc - 1,
                         reads=[wkey] + (skeys if kc == 0 else []), writes=[pk])
            kc0 += nkc
        for tt in range(ntt):
            evac_fn(tt, cb, banks[tt][0], banks[tt][1])


def stage_dense(k):
    nc, P, sb = k.nc, k.P, k.sb
    k.nrot = 8
    k.psi = 0
    with ExitStack() as st:
        G = [sb(st, "G%d" % i, [128, 16, 512], BF16) for i in range(5)]
        h = sb(st, "h", [128, 4, D], F32)
        lnp = sb(st, "lnp", [128, 2, D], F32)
        stats4 = sb(st, "stats4", [128, 4, 4, 6], F32)
        mv4 = sb(st, "mv4", [128, 4, 2], F32)
        rs4 = sb(st, "rs4", [128, 4], F32)
        mv = sb(st, "mv", [128, 2], F32)
        rs = sb(st, "rs", [128, 1], F32)
        tmpf = [sb(st, "tmpf%d" % i, [128, 512], F32) for i in range(2)]
        tmpb = [sb(st, "tmpb%d" % i, [128, 512], BF16) for i in range(2)]
        qT = sb(st, "qTx", [128, 4, 512], BF16)
        PTx = [sb(st, "PTx%d" % i, [128, 512], BF16) for i in range(4)]
        otm = sb(st, "otm", [128, 4, 512], BF16)
        oT = sb(st, "oTx", [128, 4, 512], BF16)
        memKT = sb(st, "memKT", [128, 4, 256], BF16)
        memV = sb(st, "memV", [128, 2, 4, 130], BF16)
        rd = sb(st, "rd", [128, 4], F32)

        def KG(i):
            return [("G", i, kc) for kc in range(16)]

        def Gview(i, t):
            return G[i][:, :, :].rearrange("p a b -> p (a b)").rearrange("p (t d) -> p t d", t=t)

        memb16 = Gview(0, 4)
        P.dma("pool", memb16[:, 0:2, :], k.mem.rearrange("(t p) d -> p t d", p=128), writes=KG(0))
        memT = G[1][:, :, 0:256]
        for kp in range(8):
            srcs = [memb16[:, tt, (2 * kp + kk) * 128:(2 * kp + kk + 1) * 128] for kk in range(2) for tt in range(2)]
            transpose_block(k, memT[:, 2 * kp:2 * kp + 2, :], srcs, KG(0), [("G", 1, 2 * kp), ("G", 1, 2 * kp + 1)])
        mkeys = KG(1)
        lin_fm(k, [(memT, mkeys, 16)], k.xwkv, 0, 512, 256,
               lambda nb, ps, pk: evac(k, memKT[:, nb, :], ps[:, 0:256], [pk], [("memKT", nb)]))
        P.op("dve", lambda e: e.memset(memV[:, :, :, 128:130], 1.0), writes=["memV1"])
        lin_tm(k, [(memT, mkeys, 16)], k.xwkv, 512, 512, 2,
               lambda tt, cb, ps, pk: evac(k, memV[:, tt, :, 0:128], ps[:, 0:512].rearrange("p (h d) -> p h d", h=4),
                                           [pk], [("memV", tt)], eng="dve"))
        mkv_keys = [("memKT", nb) for nb in range(4)] + [("memV", tt) for tt in range(2)] + ["memV1"]

        tfi = [0]

        def layer_norm(i, c, last):
            tok0 = c * 512
            P.dma("sp", lnp[:, 0, :], k.lng[i].broadcast_to([128, D]), writes=[("lnp", 0)])
            P.dma("sp", lnp[:, 1, :], k.lnb[i].broadcast_to([128, D]), writes=[("lnp", 1)])
            hb = Gview(0, 4)
            for tt in range(4):
                hk = ("h", tt)
                for jj in range(4):
                    P.op("dve", (lambda tt, jj: lambda e: e.bn_stats(out=stats4[:, tt, jj, :], in_=h[:, tt, jj * 512:(jj + 1) * 512]))(tt, jj),
                         reads=[hk], writes=[("stats", tt, jj)])
                P.op("dve", (lambda tt: lambda e: e.bn_aggr(out=mv4[:, tt, :], in_=stats4[:, tt, :, :].rearrange("p a b -> p (a b)")))(tt),
                     reads=[("stats", tt, jj) for jj in range(4)], writes=[("mv", tt)])
            mvk = [("mv", tt) for tt in range(4)]
            P.op("dve", lambda e: e.tensor_scalar(out=rs4[:, :], in0=mv4[:, :, 1], scalar1=EPS, scalar2=None, op0=ALU.add),
                 reads=mvk, writes=["rs4"])
            P.op("act", lambda e: e.activation(out=rs4[:, :], in_=rs4[:, :], func=AF.Sqrt), reads=["rs4"], writes=["rs4"])
            P.op("dve", lambda e: e.reciprocal(out=rs4[:, :], in_=rs4[:, :]), reads=["rs4"], writes=["rs4"])
            for tt in range(4):
                hk = ("h", tt)
                P.op("dve", (lambda tt: lambda e: e.scalar_tensor_tensor(out=h[:, tt, :], in0=h[:, tt, :], scalar=mv4[:, tt, 0:1], in1=lnp[:, 0, :],
                                                                        op0=ALU.subtract, op1=ALU.mult))(tt),
                     reads=[hk, ("mv", tt), ("lnp", 0)], writes=[hk])
                P.op("dve", (lambda tt: lambda e: e.scalar_tensor_tensor(out=h[:, tt, :], in0=h[:, tt, :], scalar=rs4[:, tt:tt + 1], in1=lnp[:, 1, :],
                                                                        op0=ALU.mult, op1=ALU.add))(tt),
                     reads=[hk, "rs4", ("lnp", 1)], writes=[hk])
                if last:
                    P.dma("sp", k.out[tok0 + tt * 128:tok0 + (tt + 1) * 128, :], h[:, tt, :], reads=[hk],
                          writes=[("out", c, tt)], semkey=("out", tt))
                else:
                    P.op("act", (lambda tt: lambda e: e.activation(out=hb[:, tt, :], in_=h[:, tt, :], func=AF.Copy))(tt),
                         reads=[hk], writes=[("G", 0, 4 * tt + q_) for q_ in range(4)])
            if not last:
                for kc in range(16):
                    transpose_block(k, G[1][:, kc, :], [hb[:, tt, kc * 128:(kc + 1) * 128] for tt in range(4)], KG(0), ("G", 1, kc))

        def resid_evac(tt, cb, ps, pk):
            P.op("dve", (lambda tt, cb, ps: lambda e: e.scalar_tensor_tensor(
                out=h[:, tt, cb * 512:(cb + 1) * 512], in0=h[:, tt, cb * 512:(cb + 1) * 512], scalar=ALPHA, in1=ps[:, 0:512],
                op0=ALU.mult, op1=ALU.add))(tt, cb, ps), reads=[pk, ("h", tt)], writes=[("h", tt)])

        for c in range(int(os.environ.get("K_NCHD", "4"))):
            tok0 = c * 512
            xsrc = k.xq[tok0:tok0 + 512, :].rearrange("(t p) d -> p t d", p=128)
            xb = Gview(0, 4)
            P.dma("pool", xb[:, :, :], xsrc, writes=KG(0))
            P.dma("sp", h[:, :, :], xsrc, writes=[("h", tt) for tt in range(4)])
            P.dma("sp", G[2][:, :, :], k.YT.ap().rearrange("b p t -> p b t")[:, :, tok0:tok0 + 512], writes=KG(2))
            for kc in range(16):
                transpose_block(k, G[1][:, kc, :], [xb[:, tt, kc * 128:(kc + 1) * 128] for tt in range(4)], KG(0), ("G", 1, kc))
            lin_fm(k, [(G[1], KG(1), 16)], k.w_in, C_GA, 2048, 512,
                   lambda nb, ps, pk: P.op("act", lambda e: e.activation(out=G[3][:, nb, :], in_=ps[:, 0:512], func=AF.Sigmoid),
                                           reads=[pk], writes=[("G", 3, nb)]))
            lin_fm(k, [(G[2][:, 0:8, :], [("G", 2, i) for i in range(8)], 8)], k.wba, 0, 2048, 512,
                   lambda nb, ps, pk: P.op("dve", lambda e: e.tensor_tensor(out=G[3][:, nb, :], in0=ps[:, 0:512], in1=G[3][:, nb, :], op=ALU.mult),
                                           reads=[pk, ("G", 3, nb)], writes=[("G", 3, nb)]))
            lin_fm(k, [(G[1], KG(1), 16)], k.w_in, C_GB, 2048, 512,
                   lambda nb, ps, pk: P.op("act", lambda e: e.activation(out=G[4][:, nb, :], in_=ps[:, 0:512], func=AF.Sigmoid),
                                           reads=[pk], writes=[("G", 4, nb)]))

            def ev_b(nb, ps, pk):
                i = tfi[0]
                tfi[0] ^= 1
                tb = tmpb[i]
                P.op("dve", lambda e: e.tensor_tensor(out=tb[:, :], in0=ps[:, 0:512], in1=G[4][:, nb, :], op=ALU.mult),
                     reads=[pk, ("G", 4, nb)], writes=[("tmpb", i)])
                P.op("dve", lambda e: e.tensor_tensor(out=G[3][:, nb, :], in0=G[3][:, nb, :], in1=tb[:, :], op=ALU.add),
                     reads=[("tmpb", i), ("G", 3, nb)], writes=[("G", 3, nb)])
            lin_fm(k, [(G[2][:, 8:16, :], [("G", 2, i) for i in range(8, 16)], 8)], k.wbb, 0, 2048, 512, ev_b)
            lin_tm(k, [(G[3], KG(3), 16)], k.wmix, 0, 2048, 4, resid_evac)
            layer_norm(0, c, False)
            lin_fm(k, [(G[1], KG(1), 16)], k.xwq, 0, 512, 512,
                   lambda nb, ps, pk: evac(k, qT[:, nb, :], ps[:, 0:512], [pk], [("qT", nb)], scale=128.0 ** -0.5))
            for hh in range(4):
                for mb in range(2):
                    S_, sk = bank(k)
                    P.mm(S_[:, 0:512], memKT[:, hh, mb * 128:(mb + 1) * 128], qT[:, hh, :], True, True,
                         reads=[("qT", hh)] + mkv_keys, writes=[sk])
                    pt = PTx[(hh % 2) * 2 + mb]
                    ptk = ("PTx", (hh % 2) * 2 + mb)
                    P.op("act", (lambda pt, S_: lambda e: e.activation(out=pt[:, :], in_=S_[:, 0:512], func=AF.Exp))(pt, S_),
                         reads=[sk], writes=[ptk])
                for tt in range(4):
                    O_, ok_ = bank(k)
                    for mb in range(2):
                        P.mm(O_[:, 0:129], PTx[(hh % 2) * 2 + mb][:, tt * 128:(tt + 1) * 128], memV[:, mb, hh, 0:129], mb == 0, mb == 1,
                             reads=[("PTx", (hh % 2) * 2 + mb)] + mkv_keys, writes=[ok_])
                    P.op("dve", (lambda O_, tt: lambda e: e.reciprocal(out=rd[:, tt:tt + 1], in_=O_[:, 128:129]))(O_, tt),
                         reads=[ok_], writes=[("rd", tt)])
                    P.op("dve", (lambda O_, tt, hh: lambda e: e.tensor_scalar(out=otm[:, tt, hh * 128:(hh + 1) * 128], in0=O_[:, 0:128],
                                                                            scalar1=rd[:, tt:tt + 1], scalar2=None, op0=ALU.mult))(O_, tt, hh),
                         reads=[ok_, ("rd", tt)], writes=[("otm", tt, hh)])
                transpose_block(k, oT[:, hh, :], [otm[:, tt, hh * 128:(hh + 1) * 128] for tt in range(4)],
                                [("otm", tt, hh) for tt in range(4)], ("oT", hh))
            lin_tm(k, [(oT, [("oT", hh) for hh in range(4)], 4)], k.xwo, 0, 2048, 4, resid_evac)
            layer_norm(1, c, False)
            hid = [G[0], G[2], G[3], G[4]]
            hidx = [0, 2, 3, 4]

            def ev_h(nb, ps, pk):
                i = tfi[0]
                tfi[0] ^= 1
                tf = tmpf[i]
                dst = hid[nb // 16][:, nb % 16, :]
                P.op("act", lambda e: e.activation(out=tf[:, :], in_=ps[:, 0:512], func=AF.Relu), reads=[pk], writes=[("tmpf", i)])
                P.op("dve", lambda e: e.tensor_tensor(out=dst, in0=tf[:, :], in1=tf[:, :], op=ALU.mult),
                     reads=[("tmpf", i)], writes=[("G", hidx[nb // 16], nb % 16)])
            lin_fm(k, [(G[1], KG(1), 16)], k.w1, 0, 8192, 512, ev_h)
            lin_tm(k, [(hid[gi], KG(hidx[gi]), 16) for gi in range(4)], k.w2, 0, 2048, 4, resid_evac)
            layer_norm(2, c, True)
        P.flush()


_CONST_CACHE = {}


def make_in_maps(inputs):
    x = np.asarray(inputs["x"], np.float32)
    maps = []
    for c in range(8):
        b, h = c // 2, c % 2
        if h not in _CONST_CACHE:
            _CONST_CACHE[h] = _host_consts(h)
        cst = _CONST_CACHE[h]
        xb = x[b].reshape(NB, 128, D)
        xq = np.ascontiguousarray(xb[h::2].reshape(2048, D))
        if h == 0:
            xf = np.concatenate([np.zeros((128, D), np.float32), x[b, :S - 128]], axis=0)
        else:
            xf = x[b]
        m = {"xq": xq, "xf": np.ascontiguousarray(xf), "mem": np.ascontiguousarray(np.asarray(inputs["mem"], np.float32)[b])}
        for name in ("w_in", "attn_sinks", "cmp_pos_k", "cmp_pos_v", "cmp_w1_k", "cmp_w1_v", "cmp_w2_k", "cmp_w2_v",
                     "w_branch_swa", "w_branch_nsa", "w_mix_out", "ln1_g", "ln1_b", "ln2_g", "ln2_b", "ln3_g", "ln3_b",
                     "xa_w_q", "xa_w_kv", "xa_w_o", "mlp_w1", "mlp_w2"):
            a = np.asarray(inputs[name], np.float32)
            a = a[0]
            if a.ndim == 1:
                a = a[None, :]
            m[name] = np.ascontiguousarray(a)
        m["rel_bias_table"] = np.ascontiguousarray(np.asarray(inputs["rel_bias_table"], np.float32))
        for n in ("swa", "sel", "win", "cmp"):
            m["oh_" + n] = cst["oh_" + n]
        for n in ("memb", "ov", "padb", "cvalid", "smul", "sadd"):
            m[n] = cst[n]
        maps.append(m)
    return maps


_NC_CACHE = {}


def kernel(**inputs):
    if "nc" not in _NC_CACHE:
        _NC_CACHE["nc"] = build()
    nc, k = _NC_CACHE["nc"]
    maps = make_in_maps(inputs)
    res = run_bass_kernel_spmd(nc, maps, core_ids=list(range(8)))
    out = np.zeros((4, S, D), np.float32)
    for c in range(8):
        b, h = c // 2, c % 2
        o = np.asarray(res.results[c]["out"], np.float32).reshape(16, 128, D)
        out[b].reshape(NB, 128, D)[h::2] = o
    return out
```
